# Optimizing a Trainium2 kernel written in Bass

```python
import jax, jax.numpy as jnp
from jax import lax
import numpy as np

D_MODEL = 1024
BATCH = 16
SEQ = 2048
DEPTH = 1
DEC_BATCH = 1
DEC_SEQ = 16384
PAST_LEN = 128

GRID_W = 64
ATTN_HEADS = 8
ATTN_KV_HEADS = 2
ATTN_HEAD_DIM = 64
RET_HEADS = 4
RET_KEY_DIM = 128
RET_VALUE_DIM = 256
ATTN_Q_W = ATTN_HEADS * ATTN_HEAD_DIM
ATTN_KV_W = ATTN_KV_HEADS * ATTN_HEAD_DIM
RET_QK_W = RET_HEADS * RET_KEY_DIM
RET_V_W = RET_HEADS * RET_VALUE_DIM
IN_PROJ_W = ATTN_Q_W + 2 * ATTN_KV_W + 2 * RET_QK_W + 2 * RET_V_W + 2 * D_MODEL
D_FF = -(-8 * D_MODEL // (3 * 256)) * 256
Q_BLOCK = 128
RET_CHUNK = 128
ROPE_THETA = 10000.0
EPS = 1e-6

kernel_name = "hybrid_gqa_axial_retention_encoder"


def rms_norm(x, gain):
    xf = x.astype(jnp.float32)
    y = xf * lax.rsqrt(jnp.mean(xf * xf, axis=-1, keepdims=True) + EPS)
    return (y * gain.astype(jnp.float32)).astype(x.dtype)


def axial_rope_tables(n_tokens, head_dim):
    n_rows = n_tokens // GRID_W
    row = jnp.repeat(jnp.arange(n_rows, dtype=jnp.float32), GRID_W)
    col = jnp.tile(jnp.arange(GRID_W, dtype=jnp.float32), n_rows)
    n_freq = head_dim // 4
    inv_freq = ROPE_THETA ** (-jnp.arange(n_freq, dtype=jnp.float32) / n_freq)
    ang = jnp.concatenate([row[:, None] * inv_freq, col[:, None] * inv_freq], axis=-1)
    return jnp.cos(ang), jnp.sin(ang)


def apply_rope(x, cos, sin):
    xf = x.astype(jnp.float32).reshape(x.shape[:-1] + (x.shape[-1] // 2, 2))
    c = cos[None, :, None, :]
    s = sin[None, :, None, :]
    x0, x1 = xf[..., 0], xf[..., 1]
    out = jnp.stack([x0 * c - x1 * s, x0 * s + x1 * c], axis=-1)
    return out.reshape(x.shape).astype(x.dtype)


def axial_gqa_attention(q, k, v, q_gain, k_gain):
    B, T = q.shape[0], q.shape[1]
    cos, sin = axial_rope_tables(T, ATTN_HEAD_DIM)
    q = apply_rope(rms_norm(q, q_gain), cos, sin)
    k = apply_rope(rms_norm(k, k_gain), cos, sin)
    group = ATTN_HEADS // ATTN_KV_HEADS
    scale = ATTN_HEAD_DIM ** -0.5
    n_blocks = T // Q_BLOCK
    q_blocks = q.reshape(B, n_blocks, Q_BLOCK, ATTN_KV_HEADS, group, ATTN_HEAD_DIM).transpose(1, 0, 2, 3, 4, 5)

    def one_block(qb):
        s = jnp.einsum('bqkgd,bskd->bkgqs', qb, k, preferred_element_type=jnp.float32) * scale
        p = jax.nn.softmax(s, axis=-1).astype(v.dtype)
        return jnp.einsum('bkgqs,bskd->bqkgd', p, v)

    out = lax.map(one_block, q_blocks)
    return out.transpose(1, 0, 2, 3, 4, 5).reshape(B, T, ATTN_Q_W)


def retention_one_direction(q, k, v, log_gamma, strict):
    B, T, H, dk = q.shape
    dv = v.shape[-1]
    C = RET_CHUNK
    N = T // C
    qc = q.reshape(B, N, C, H, dk)
    kc = k.reshape(B, N, C, H, dk)
    vc = v.reshape(B, N, C, H, dv)
    pos = jnp.arange(C, dtype=jnp.float32)
    diff = pos[:, None] - pos[None, :]
    mask = (diff > 0) if strict else (diff >= 0)
    decay_intra = jnp.where(mask[None], jnp.exp(log_gamma[:, None, None] * jnp.maximum(diff, 0.0)[None]), 0.0)
    scores = jnp.einsum('bnihd,bnjhd->bnhij', qc, kc) * decay_intra
    intra = jnp.einsum('bnhij,bnjhe->bnihe', scores, vc)
    k_decay = jnp.exp(log_gamma[None, :] * (C - 1 - pos)[:, None])
    chunk_kv = jnp.einsum('bnjhd,bnjhe->nbhde', kc * k_decay[:, :, None], vc)
    chunk_decay = jnp.exp(log_gamma * C)[:, None, None]

    def step(state, kv):
        return chunk_decay * state + kv, state

    _, prev_states = lax.scan(step, jnp.zeros((B, H, dk, dv), jnp.float32), chunk_kv)
    q_decay = jnp.exp(log_gamma[None, :] * (pos + 1.0)[:, None])
    cross = jnp.einsum('bnihd,nbhde->bnihe', qc * q_decay[:, :, None], prev_states)
    return (intra + cross).reshape(B, T, H, dv)


def bidirectional_retention(q, k, v, gate, decay_fwd, decay_bwd, norm_gain):
    B, T = q.shape[0], q.shape[1]
    cos, sin = axial_rope_tables(T, RET_KEY_DIM)
    qf = apply_rope(q, cos, sin).astype(jnp.float32) * (RET_KEY_DIM ** -0.5)
    kf = apply_rope(k, cos, sin).astype(jnp.float32)
    vf = v.astype(jnp.float32)
    fwd = retention_one_direction(qf, kf, vf, jax.nn.log_sigmoid(decay_fwd.astype(jnp.float32)), False)
    bwd = retention_one_direction(qf[:, ::-1], kf[:, ::-1], vf[:, ::-1],
                                  jax.nn.log_sigmoid(decay_bwd.astype(jnp.float32)), True)[:, ::-1]
    y = fwd + bwd
    mean = jnp.mean(y, axis=-1, keepdims=True)
    var = jnp.mean(jnp.square(y - mean), axis=-1, keepdims=True)
    y = ((y - mean) * lax.rsqrt(var + EPS)).reshape(B, T, RET_V_W) * norm_gain.astype(jnp.float32)
    return (jax.nn.silu(gate.astype(jnp.float32)) * y).astype(gate.dtype)


def encoder_layer(x, norm_mix, w_in, b_gate, q_norm, k_norm, ret_decay_fwd, ret_decay_bwd,
                  ret_norm, w_branch_attn, w_branch_ret, w_out, norm_ffn, w_ffn_in, w_ffn_out):
    B, T, _ = x.shape
    h = rms_norm(x, norm_mix)
    proj = h @ w_in
    widths = [ATTN_Q_W, ATTN_KV_W, ATTN_KV_W, RET_QK_W, RET_QK_W, RET_V_W, RET_V_W]
    q_a, k_a, v_a, q_r, k_r, v_r, g_r, gate_logits = jnp.split(proj, list(np.cumsum(widths)), axis=-1)
    attn = axial_gqa_attention(q_a.reshape(B, T, ATTN_HEADS, ATTN_HEAD_DIM),
                               k_a.reshape(B, T, ATTN_KV_HEADS, ATTN_HEAD_DIM),
                               v_a.reshape(B, T, ATTN_KV_HEADS, ATTN_HEAD_DIM), q_norm, k_norm)
    ret = bidirectional_retention(q_r.reshape(B, T, RET_HEADS, RET_KEY_DIM),
                                  k_r.reshape(B, T, RET_HEADS, RET_KEY_DIM),
                                  v_r.reshape(B, T, RET_HEADS, RET_VALUE_DIM),
                                  g_r, ret_decay_fwd, ret_decay_bwd, ret_norm)
    gates = jax.nn.sigmoid((gate_logits + b_gate).astype(jnp.float32)).astype(x.dtype)
    g_attn, g_ret = jnp.split(gates, 2, axis=-1)
    mixed = g_attn * (attn @ w_branch_attn) + g_ret * (ret @ w_branch_ret)
    x = x + mixed @ w_out
    h = rms_norm(x, norm_ffn)
    gt, up = jnp.split(h @ w_ffn_in, 2, axis=-1)
    return x + (jax.nn.silu(gt) * up) @ w_ffn_out


def run_trunk(x, norm_mix, w_in, b_gate, q_norm, k_norm, ret_decay_fwd, ret_decay_bwd, ret_norm,
              w_branch_attn, w_branch_ret, w_out, norm_ffn, w_ffn_in, w_ffn_out, norm_final):
    for l in range(DEPTH):
        x = encoder_layer(x, norm_mix[l], w_in[l], b_gate[l], q_norm[l], k_norm[l],
                          ret_decay_fwd[l], ret_decay_bwd[l], ret_norm[l], w_branch_attn[l],
                          w_branch_ret[l], w_out[l], norm_ffn[l], w_ffn_in[l], w_ffn_out[l])
    return rms_norm(x, norm_final)


def setup_inputs(seed: int = 0) -> dict:
    key = jax.random.key(seed)
    ks = jax.random.split(key, 20)
    f32 = jnp.float32

    def w(k, shape, fan_in):
        return jax.random.normal(k, shape, f32) * (fan_in ** -0.5)

    def gain(k, shape):
        return 1.0 + 0.02 * jax.random.normal(k, shape, f32)

    base_logit = jnp.log(jnp.exp2(5.0 + jnp.arange(RET_HEADS, dtype=f32)) - 1.0)
    return {
        "x_prompt": jax.random.normal(ks[0], (BATCH, SEQ, D_MODEL), f32),
        "x_sample": jax.random.normal(ks[1], (DEC_BATCH, DEC_SEQ, D_MODEL), f32),
        "norm_mix": gain(ks[2], (DEPTH, D_MODEL)),
        "w_in": w(ks[3], (DEPTH, D_MODEL, IN_PROJ_W), D_MODEL),
        "b_gate": 0.01 * jax.random.normal(ks[4], (DEPTH, 2 * D_MODEL), f32),
        "q_norm": gain(ks[5], (DEPTH, ATTN_HEAD_DIM)),
        "k_norm": gain(ks[6], (DEPTH, ATTN_HEAD_DIM)),
        "ret_decay_fwd": base_logit + 0.01 * jax.random.normal(ks[7], (DEPTH, RET_HEADS), f32),
        "ret_decay_bwd": base_logit + 0.01 * jax.random.normal(ks[8], (DEPTH, RET_HEADS), f32),
        "ret_norm": gain(ks[9], (DEPTH, RET_V_W)),
        "w_branch_attn": w(ks[10], (DEPTH, ATTN_Q_W, D_MODEL), ATTN_Q_W),
        "w_branch_ret": w(ks[11], (DEPTH, RET_V_W, D_MODEL), RET_V_W),
        "w_out": w(ks[12], (DEPTH, D_MODEL, D_MODEL), D_MODEL),
        "norm_ffn": gain(ks[13], (DEPTH, D_MODEL)),
        "w_ffn_in": w(ks[14], (DEPTH, D_MODEL, 2 * D_FF), D_MODEL),
        "w_ffn_out": w(ks[15], (DEPTH, D_FF, D_MODEL), D_FF),
        "norm_final": gain(ks[16], (D_MODEL,)),
    }


def reference(x_prompt, x_sample, norm_mix, w_in, b_gate, q_norm, k_norm, ret_decay_fwd, ret_decay_bwd,
              ret_norm, w_branch_attn, w_branch_ret, w_out, norm_ffn, w_ffn_in, w_ffn_out, norm_final):
    y_prompt = run_trunk(x_prompt, norm_mix, w_in, b_gate, q_norm, k_norm, ret_decay_fwd, ret_decay_bwd,
                         ret_norm, w_branch_attn, w_branch_ret, w_out, norm_ffn, w_ffn_in, w_ffn_out, norm_final)
    y_sample = run_trunk(x_sample, norm_mix, w_in, b_gate, q_norm, k_norm, ret_decay_fwd, ret_decay_bwd,
                         ret_norm, w_branch_attn, w_branch_ret, w_out, norm_ffn, w_ffn_in, w_ffn_out, norm_final)
    return (y_prompt, y_sample)
```

```python
import math
import numpy as np
from contextlib import ExitStack
import concourse.bass as bass
import concourse.mybir as mybir
from concourse.bass_utils import run_bass_kernel_spmd

F32 = mybir.dt.float32
BF16 = mybir.dt.bfloat16
AF = mybir.ActivationFunctionType
ALU = mybir.AluOpType
AX = mybir.AxisListType

D = 1024
GW = 64
NH_A, NKV, DH = 8, 2, 64
NH_R, DK, DV = 4, 128, 256
DFF = 2816
NFC = DFF // 128
EPS = 1e-6
THETA = 10000.0
G = 512
NT = G // 128
BIG = 1.0e9

WIN_BLOCKS = [("qa", 0, 512), ("kva", 512, 256), ("qr", 768, 512), ("kr", 1280, 512),
              ("vr0", 1792, 512), ("vr1", 2304, 512), ("gr0", 2816, 512), ("gr1", 3328, 512),
              ("gt0", 3840, 512), ("gt1", 4352, 512), ("gt2", 4864, 512), ("gt3", 5376, 512)]
WIN_IDX = {n: i for i, (n, _, _) in enumerate(WIN_BLOCKS)}


class T:
    def __init__(self, ap, name=""):
        self.ap = ap
        self.name = name
        self.last_w = None
        self.readers = []
        self.dead = False
        self.excl = False

    def __getitem__(self, k):
        return self.ap[k]


class Op:
    __slots__ = ("eng", "fn", "deps", "raw", "id", "is_dma", "sig", "semval")

    def __init__(self, eng, fn, is_dma):
        self.eng = eng
        self.fn = fn
        self.deps = set()
        self.raw = set()
        self.is_dma = is_dma
        self.sig = False
        self.semval = 0


class Sched:
    ENGS = ("pe", "act", "dve", "pool", "sp")

    def __init__(self, nc):
        self.nc = nc
        self.ops = []

    def op(self, eng, fn, reads=(), writes=(), dma=False):
        o = Op(eng, fn, dma)
        o.id = len(self.ops)
        for t in reads:
            assert not t.dead, ("read of dead buffer", t.name)
            if t.last_w is not None:
                o.deps.add(t.last_w)
                o.raw.add(t.last_w)
            if t.excl:
                for r in t.readers:
                    if self.ops[r].eng != eng:
                        o.deps.add(r)
        for t in writes:
            assert not t.dead, ("write of dead buffer", t.name)
            if t.last_w is not None:
                o.deps.add(t.last_w)
            o.deps.update(t.readers)
        for t in reads:
            t.readers.append(o.id)
        for t in writes:
            t.last_w = o.id
            t.readers = []
        o.deps.discard(o.id)
        self.ops.append(o)
        return o

    def pe(self, fn, reads=(), writes=()):
        return self.op("pe", fn, reads, writes)

    def act(self, fn, reads=(), writes=()):
        return self.op("act", fn, reads, writes)

    def dve(self, fn, reads=(), writes=()):
        return self.op("dve", fn, reads, writes)

    def pool(self, fn, reads=(), writes=()):
        return self.op("pool", fn, reads, writes)

    def dma(self, fn, reads=(), writes=(), q="sp"):
        return self.op(q, fn, reads, writes, dma=True)

    def emit(self, stack):
        nc = self.nc
        ops = self.ops
        engh = {"pe": nc.tensor, "act": nc.scalar, "dve": nc.vector, "pool": nc.gpsimd, "sp": nc.sync}
        waits = [None] * len(ops)
        last_waited = {e: {p: -1 for p in self.ENGS} for e in self.ENGS}
        dma_waited = {e: set() for e in self.ENGS}
        for o in ops:
            need = {}
            dl = []
            for d in o.deps:
                p = ops[d]
                if p.is_dma:
                    if d not in dma_waited[o.eng]:
                        dl.append(d)
                        dma_waited[o.eng].add(d)
                    continue
                if p.eng == o.eng:
                    if o.eng == "pe":
                        continue
                if d > need.get(p.eng, -1):
                    need[p.eng] = d
            w = []
            for pe_, d in need.items():
                if d > last_waited[o.eng][pe_]:
                    last_waited[o.eng][pe_] = d
                    w.append(d)
                    ops[d].sig = True
            for d in dl:
                w.append(d)
            waits[o.id] = w
        esem = {e: stack.enter_context(nc.semaphore("s_" + e)) for e in ("pe", "act", "dve", "pool")}
        ecnt = {e: 0 for e in esem}
        NQ = {"sp": 24, "pool": 8, "act": 8}
        dsem = {q: [stack.enter_context(nc.semaphore("d%s%d" % (q, i))) for i in range(n)] for q, n in NQ.items()}
        dcnt = {q: [0] * n for q, n in NQ.items()}
        dk = {q: 0 for q in NQ}
        dslot = {}
        for o in ops:
            if o.is_dma:
                q = o.eng
                s_ = dk[q] % NQ[q]
                dk[q] += 1
                dcnt[q][s_] += 16
                dslot[o.id] = (q, s_, dcnt[q][s_])
            elif o.sig:
                ecnt[o.eng] += 1
                o.semval = ecnt[o.eng]
        self.n_waits = 0
        for o in ops:
            h = engh[o.eng]
            for d in waits[o.id]:
                p = ops[d]
                if p.is_dma:
                    q, s_, v = dslot[d]
                    h.wait_ge(dsem[q][s_], v)
                else:
                    h.wait_ge(esem[p.eng], p.semval)
                self.n_waits += 1
            if o.is_dma:
                q, s_, v = dslot[o.id]
                if v > 16:
                    h.wait_ge(dsem[q][s_], v - 16)
                ins = o.fn(h)
                ins.then_inc(dsem[q][s_], 16)
            else:
                ins = o.fn(h)
                if o.sig:
                    ins.then_inc(esem[o.eng], 1)
        for q in NQ:
            for s_ in range(NQ[q]):
                if dcnt[q][s_] > 0:
                    nc.sync.wait_ge(dsem[q][s_], dcnt[q][s_])
        self.stats = dict(ecnt)


class Arena:
    def __init__(self, tensor, nbytes):
        self.t = tensor
        self.n = nbytes
        self.live = []

    def carve(self, lo, nbytes, name):
        hi = lo + nbytes
        assert hi <= self.n and lo % 4 == 0 and nbytes % 4 == 0, (name, lo, nbytes, self.n)
        t = T(self.t[:, lo // 2:hi // 2], name)
        inh = set()
        keep = []
        for (a, b, o) in self.live:
            if a < hi and lo < b:
                o.dead = True
                if o.last_w is not None:
                    inh.add(o.last_w)
                inh.update(o.readers)
                if a < lo:
                    keep.append((a, lo, o))
                if hi < b:
                    keep.append((hi, b, o))
            else:
                keep.append((a, b, o))
        keep.append((lo, hi, t))
        self.live = keep
        t.readers = sorted(inh)
        return t


class Psum:
    def __init__(self, tensor):
        self.t = tensor
        self.cur = [None] * 8
        self.rr = 0

    def bank(self, i, name="ps"):
        t = T(self.t[:, i, :], name)
        t.excl = True
        o = self.cur[i]
        if o is not None:
            o.dead = True
            inh = set(o.readers)
            if o.last_w is not None:
                inh.add(o.last_w)
            t.readers = sorted(inh)
        self.cur[i] = t
        t.idx = i
        return t

    def next(self, name="ps", pool=(0, 1, 2, 3, 4, 5, 6, 7)):
        i = pool[self.rr % len(pool)]
        self.rr += 1
        return self.bank(i, name)


def build(cfg):
    TP, NP, TS, SEG = cfg["TP"], cfg["NP"], cfg["TS"], cfg["SEG"]
    assert TP % G == 0 and SEG % G == 0 and TS % G == 0
    NKMAX = max(TP, TS)
    NCHMAX = max(TP, SEG) // 128
    nc = bass.Bass("TRN2", target_bir_lowering=False)

    def din(name, shape, dt=F32):
        return nc.dram_tensor(name, list(shape), dt, kind="ExternalInput").ap()

    def dscr(name, shape, dt=BF16):
        return nc.dram_tensor(name, list(shape), dt, kind="Internal").ap()

    xp = din("xp", [NP, TP, D])
    xs = din("xs", [TS, D])
    rcp = din("rcp", [128, TP // 128, 2])
    rcs = din("rcs", [128, TS // 128, 2])
    dists = din("dists", [128, TS // 128, 2])
    ctab = din("ctab", [128, 802])
    w_in = din("w_in", [D, 5888])
    b_gate = din("b_gate", [1, 2048])
    q_norm = din("q_norm", [1, DH])
    k_norm = din("k_norm", [1, DH])
    dec_f = din("dec_f", [1, 4])
    dec_b = din("dec_b", [1, 4])
    ret_norm = din("ret_norm", [1, D])
    w_ba = din("w_ba", [512, D])
    w_br = din("w_br", [D, D])
    w_out = din("w_out", [D, D])
    norm_mix = din("norm_mix", [1, D])
    norm_ffn = din("norm_ffn", [1, D])
    norm_fin = din("norm_fin", [1, D])
    w_f1 = din("w_f1", [D, 2 * DFF])
    w_f2 = din("w_f2", [DFF, D])
    yp = nc.dram_tensor("yp", [NP, TP, D], F32, kind="ExternalOutput").ap()
    ys = nc.dram_tensor("ys", [SEG, D], F32, kind="ExternalOutput").ap()

    wA = dscr("wA", [12, 128, 8 * 512])
    w1s = dscr("w1s", [11, 128, 2 * 8 * 256])
    w2s = dscr("w2s", [2, 128, NFC * 512])
    wbrs = dscr("wbrs", [2, 128, 8 * 512])
    wouts = dscr("wouts", [2, 128, 8 * 512])
    wbas = dscr("wbas", [2, 64, 8 * 512])
    tabp = dscr("tabp_s", [TP // 128, 128, 192], F32)
    tabs = dscr("tabs_s", [TS // 128, 128, 192], F32)
    kts = dscr("kts", [128, NKMAX])
    vs = dscr("vs", [NKV, 128, (NKMAX // 128) * 66])
    bns = dscr("bns", [NCHMAX, 128, 1024])

    st = ExitStack()
    S = Sched(nc)

    class StopBuild(Exception):
        pass

    def ck(name):
        if cfg.get("stop") == name:
            raise StopBuild()
    ARENA_BYTES = 206 * 1024
    arena_t = st.enter_context(nc.sbuf_tensor("arena", [128, ARENA_BYTES // 2], BF16))
    AR = Arena(arena_t, ARENA_BYTES)
    psum_t = st.enter_context(nc.psum_tensor("psum", [128, 8, 512], F32))
    PS = Psum(psum_t)
    KB = 1024

    def f32v(buf_, pat=None, **kw):
        v = buf_.ap.bitcast(F32)
        return v.rearrange(pat, **kw) if pat else v

    def b16v(buf_, pat=None, **kw):
        v = buf_.ap
        return v.rearrange(pat, **kw) if pat else v

    off = [0]

    def res(nbytes, name):
        t = AR.carve(off[0], nbytes, name)
        off[0] += nbytes
        return t

    g_mix = res(4 * KB, "g_mix")
    g_ffn = res(4 * KB, "g_ffn")
    g_fin = res(4 * KB, "g_fin")
    g_ret = res(4 * KB, "g_ret")
    maskT = res(2 * KB, "maskT")
    QDf = res(2 * KB, "QDf")
    QDb = res(2 * KB, "QDb")
    ctb = res(3208, "ctab")
    ident = res(256, "ident")
    ones_b = res(256, "ones_b")
    ones_f = res(256, "ones_f")
    gq = res(256, "gq")
    gk = res(256, "gk")
    small = res(1024, "small")
    bgb = res(4 * KB, "bgb")
    Ff = res(4 * KB, "Ff")
    Bf = res(4 * KB, "Bf")
    Fbf = [res(2 * KB, "Fbf0"), res(2 * KB, "Fbf1")]
    Bbf = [res(2 * KB, "Bbf0"), res(2 * KB, "Bbf1")]
    stats = res(1024, "stats")
    statsF = [res(256, "statsF0"), res(256, "statsF1")]
    RES_END = off[0]
    GB = (RES_END + 1023) // 1024 * 1024

    smallv = f32v(small)
    C_LGF, C_LGB, C_CDF, C_CDB, C_KDF, C_KDB, C_NB, C_NH, C_TMP = 0, 4, 8, 12, 16, 20, 24, 32, 64

    O_XT = 0
    O_TAB = 16 * KB
    O_DST = 19 * KB
    O_WS = [20 * KB, 28 * KB]
    O_HT = 36 * KB
    O_HB = 44 * KB
    O_P = 52 * KB
    O_QT = O_P
    O_QRT = O_P + 8 * KB
    O_QFT = O_P + 12 * KB
    O_QBT = O_P + 16 * KB
    O_KRT = O_P + 20 * KB
    O_KF = O_P + 24 * KB
    O_KB2 = O_P + 12 * KB
    O_VR = O_P + 28 * KB
    O_GSIL = O_P + 36 * KB
    O_BN = O_P + 44 * KB
    O_ATT = O_P + 52 * KB
    O_RETT = O_P + 60 * KB
    O_E = O_P + 68 * KB
    O_CT = O_E + 6 * KB
    O_TG = O_P
    O_M1 = O_P + 16 * KB
    O_MIX = O_P + 44 * KB
    O_MIXT = O_P + 28 * KB
    O_HID = O_P
    O_FT = O_P + 22 * KB
    O_W2 = [O_P + 30 * KB, O_P + 52 * KB]
    O_OUT = [O_P + 74 * KB, O_P + 78 * KB]
    O_STG_F = [O_P + 52 * KB]
    O_XT2 = O_P + 36 * KB
    O_TAB2 = O_P + 100 * KB
    O_STG_B = [O_P, O_P]
    O_HT2 = O_P + 92 * KB
    assert GB + O_P + 103 * KB <= ARENA_BYTES, (GB, O_P)

    def gcarve(o, n, name):
        return AR.carve(GB + o, n, name)

    def mm(out, lhsT, rhs, start, stop, reads, writes):
        S.pe(lambda e, o=out, l=lhsT, r=rhs, a=start, b=stop: e.matmul(o, lhsT=l, rhs=r, start=a, stop=b),
             reads, writes)

    def tr(out, in_, reads, writes):
        S.pe(lambda e, o=out, i=in_: e.transpose(out=o, in_=i, identity=ident[:]), list(reads) + [ident], writes)

    def act(out, in_, func, reads, writes, scale=1.0, bias=None, accum=None):
        def f(e, o=out, i=in_, fu=func, sc=scale, bi=bias, ac=accum):
            kw = {}
            if bi is not None:
                kw["bias"] = bi
            if ac is not None:
                kw["accum_out"] = ac
            return e.activation(out=o, in_=i, func=fu, scale=sc, **kw)
        S.act(f, reads, writes)

    def tt(eng, out, in0, in1, op, reads, writes):
        S.op(eng, lambda e, o=out, a=in0, b=in1, p=op: e.tensor_tensor(out=o, in0=a, in1=b, op=p), reads, writes)

    def ts(eng, out, in0, s1, op0, reads, writes, s2=None, op1=None):
        if op1 is None:
            S.op(eng, lambda e, o=out, a=in0, x=s1, p=op0: e.tensor_scalar(out=o, in0=a, scalar1=x, scalar2=None, op0=p),
                 reads, writes)
        else:
            S.op(eng, lambda e, o=out, a=in0, x=s1, y=s2, p=op0, q=op1:
                 e.tensor_scalar(out=o, in0=a, scalar1=x, scalar2=y, op0=p, op1=q), reads, writes)

    def stt(out, in0, scalar, in1, op0, op1, reads, writes):
        S.dve(lambda e, o=out, a=in0, s_=scalar, b=in1, p=op0, q=op1:
              e.scalar_tensor_tensor(out=o, in0=a, scalar=s_, in1=b, op0=p, op1=q), reads, writes)

    def cp(eng, out, in_, reads, writes):
        if eng == "act":
            act(out, in_, AF.Copy, reads, writes)
        else:
            S.op(eng, lambda e, o=out, i=in_: e.tensor_copy(out=o, in_=i), reads, writes)

    def dma(out, in_, reads, writes, q="sp"):
        S.dma(lambda e, o=out, i=in_: e.dma_start(out=o, in_=i), reads, writes, q=q)

    def memset(eng, ap, val, writes):
        S.op(eng, lambda e, a=ap, v=val: e.memset(a, v), (), writes)

    def setup():
        for (gt, src) in ((g_mix, norm_mix), (g_ffn, norm_ffn), (g_fin, norm_fin), (g_ret, ret_norm)):
            dma(f32v(gt), src.partition_broadcast(128), [], [gt])
        ts("pool", f32v(g_ret), f32v(g_ret), 0.5, ALU.mult, [g_ret], [g_ret])
        dma(f32v(ctb), ctab[:, :], [], [ctb])
        dma(f32v(gq), q_norm.partition_broadcast(128), [], [gq])
        dma(f32v(gk), k_norm.partition_broadcast(128), [], [gk])
        dma(smallv[:, C_LGF:C_LGF + 4], dec_f.partition_broadcast(128), [], [small])
        dma(smallv[:, C_LGB:C_LGB + 4], dec_b.partition_broadcast(128), [], [small])
        ck("s1")
        tmpf = gcarve(O_CT, 2 * KB, "tmp_ident")
        tf = f32v(tmpf)[:, 0:128]
        memset("pool", tf, 1.0, [tmpf])
        S.pool(lambda e: e.affine_select(out=tf, in_=tf, pattern=[[-1, 128]], compare_op=ALU.is_equal, fill=0.0,
                                         base=0, channel_multiplier=1), [tmpf], [tmpf])
        cp("dve", ident[:], tf, [tmpf], [ident])
        memset("pool", ones_b[:], 0.0, [ones_b])
        memset("pool", ones_b[0:1, :], 1.0, [ones_b])
        memset("pool", f32v(ones_f), 0.0, [ones_f])
        memset("pool", f32v(ones_f)[64:65, :], 1.0, [ones_f])
        memset("pool", bgb[:], 0.0, [bgb])
        memset("pool", smallv[:, C_NH:C_NH + 32], -0.5, [small])
        ck("s2")
        tb = gcarve(O_CT + 2 * KB, 8 * KB, "tmp_bg")
        dma(f32v(tb)[0:1, :], b_gate[:, :], [], [tb])
        cp("dve", bgb[0:1, :], f32v(tb)[0:1, :], [tb], [bgb])
        ck("s3")
        lg = smallv[:, C_LGF:C_LGF + 8]
        act(lg, lg, AF.Exp, [small], [small], scale=-1.0)
        ts("dve", lg, lg, 1.0, ALU.add, [small], [small])
        act(lg, lg, AF.Ln, [small], [small])
        ts("dve", lg, lg, -1.0, ALU.mult, [small], [small])
        ck("s4")
        cv = f32v(ctb)
        P1, P2, MGE, MLT = cv[:, 0:128], cv[:, 128:256], cv[:, 256:384], cv[:, 384:512]
        IR1, IR2, JC1, JC2 = cv[:, 512:640], cv[:, 640:768], cv[:, 768:769], cv[:, 769:770]
        act(smallv[:, C_CDF:C_CDF + 8], lg, AF.Exp, [small], [small], scale=128.0)
        act(smallv[:, C_KDF:C_KDF + 4], smallv[:, C_LGF:C_LGF + 4], AF.Exp, [small, ctb], [small], scale=JC1)
        act(smallv[:, C_KDB:C_KDB + 4], smallv[:, C_LGB:C_LGB + 4], AF.Exp, [small, ctb], [small], scale=JC2)
        ck("s5")
        mv = f32v(maskT, "p (h i) -> p h i", h=4)
        qf = f32v(QDf, "p (h i) -> p h i", h=4)
        qb = f32v(QDb, "p (h i) -> p h i", h=4)
        t1 = gcarve(O_CT + 10 * KB, 2 * KB, "tmp_m1")
        t1v = f32v(t1)[:, 0:128]
        for h in range(4):
            lf = smallv[:, C_LGF + h:C_LGF + h + 1]
            lb = smallv[:, C_LGB + h:C_LGB + h + 1]
            act(mv[:, h, :], P1, AF.Exp, [ctb, small], [maskT], scale=lf)
            tt("dve", mv[:, h, :], mv[:, h, :], MGE, ALU.mult, [maskT, ctb], [maskT])
            act(t1v, P2, AF.Exp, [ctb, small], [t1], scale=lb)
            tt("dve", t1v, t1v, MLT, ALU.mult, [t1, ctb], [t1])
            tt("dve", mv[:, h, :], mv[:, h, :], t1v, ALU.add, [maskT, t1], [maskT])
            act(qf[:, h, :], IR1, AF.Exp, [ctb, small], [QDf], scale=lf)
            act(qb[:, h, :], IR2, AF.Exp, [ctb, small], [QDb], scale=lb)
            ts("dve", qf[:, h, :], qf[:, h, :], float(DK) ** -0.5, ALU.mult, [QDf], [QDf])
            ts("dve", qb[:, h, :], qb[:, h, :], float(DK) ** -0.5, ALU.mult, [QDb], [QDb])
        ck("s6")
        mq = smallv[:, C_TMP:C_TMP + 1]
        mk = smallv[:, C_TMP + 1:C_TMP + 2]
        S.dve(lambda e: e.tensor_reduce(out=mq, in_=f32v(gq), axis=AX.X, op=ALU.max, apply_absolute_value=True),
              [gq], [small])
        S.dve(lambda e: e.tensor_reduce(out=mk, in_=f32v(gk), axis=AX.X, op=ALU.max, apply_absolute_value=True),
              [gk], [small])
        tt("dve", mq, mq, mk, ALU.mult, [small], [small])
        ts("dve", smallv[:, C_NB:C_NB + 1], mq, -8.0, ALU.mult, [small], [small])

    wT = {}
    prep_tasks = []
    stg_i = [0]
    cast_engs = ["act", "dve"]

    def prep_piece(key, src_ap, dst_ap, npart, nelem):
        st_ = {}

        def load():
            i = stg_i[0]
            stg_i[0] += 1
            sf = gcarve(O_STG_F[i % len(O_STG_F)], 16 * KB, "stg_f")
            a, b = src_ap.shape[1], src_ap.shape[2]
            fv = f32v(sf)[0:npart, 0:nelem]
            dma(fv.rearrange("p (a b) -> p a b", a=a), src_ap, [], [sf])
            st_["i"], st_["sf"], st_["fv"] = i, sf, fv

        def finish():
            i, sf, fv = st_["i"], st_["sf"], st_["fv"]
            sb_ = gcarve(O_STG_B[i % 2], 8 * KB, "stg_b")
            cp(cast_engs[i % 2], sb_[0:npart, 0:nelem], fv, [sf], [sb_])
            t = T(dst_ap, key)
            dma(dst_ap, sb_[0:npart, 0:nelem], [sb_], [t], q=("act" if cast_engs[i % 2] == "act" else "pool"))
            wT.setdefault(key, []).append(t)
        prep_tasks.append((load, finish))

    def prep_qa():
        st_ = {}

        def load():
            i = stg_i[0]
            stg_i[0] += 1
            sf = gcarve(O_STG_F[i % len(O_STG_F)], 16 * KB, "stg_f")
            fv = f32v(sf)[:, 0:4096]
            f5 = fv.rearrange("p (c h g d) -> p c h g d", c=8, h=4, g=2)
            for g in range(2):
                for h in range(4):
                    c0 = g * 256 + h * 64
                    dma(f5[:, :, h, g, :], w_in[:, c0:c0 + 64].rearrange("(c p) d -> p c d", p=128), [], [sf])
            st_["i"], st_["sf"], st_["fv"] = i, sf, fv

        def finish():
            i, sf, fv = st_["i"], st_["sf"], st_["fv"]
            sb_ = gcarve(O_STG_B[i % 2], 8 * KB, "stg_b")
            cp(cast_engs[i % 2], sb_[:, 0:4096], fv, [sf], [sb_])
            dst = wA[WIN_IDX["qa"], :, :]
            t = T(dst, "qa")
            dma(dst, sb_[:, 0:4096], [sb_], [t], q=("act" if cast_engs[i % 2] == "act" else "pool"))
            wT.setdefault("qa", []).append(t)
        prep_tasks.append((load, finish))

    def make_prep(order):
        for name in order:
            if name == "qa":
                prep_qa()
            elif name in WIN_IDX:
                bi = WIN_IDX[name]
                _, c0, w = WIN_BLOCKS[bi]
                prep_piece(name, w_in[:, c0:c0 + w].rearrange("(c p) n -> p c n", p=128), wA[bi, :, 0:8 * w], 128, 8 * w)
            elif name.startswith("w1_"):
                fb = int(name[3:])
                for gu in range(2):
                    c0 = gu * DFF + fb * 256
                    prep_piece(name, w_f1[:, c0:c0 + 256].rearrange("(c p) n -> p c n", p=128),
                               w1s[fb, :, gu * 2048:(gu + 1) * 2048], 128, 2048)
            elif name.startswith("w2_"):
                hf = int(name[3:])
                for (f0, f1) in ((0, 6), (6, 12), (12, 17), (17, 22)):
                    prep_piece(name, w_f2[f0 * 128:f1 * 128, hf * 512:(hf + 1) * 512].rearrange("(f p) n -> p f n", p=128),
                               w2s[hf, :, f0 * 512:f1 * 512], 128, (f1 - f0) * 512)
            elif name.startswith("wbr_") or name.startswith("wout_"):
                hf = int(name.split("_")[1])
                src, dst = (w_br, wbrs) if name.startswith("wbr_") else (w_out, wouts)
                prep_piece(name, src[:, hf * 512:(hf + 1) * 512].rearrange("(c p) n -> p c n", p=128),
                           dst[hf, :, :], 128, 4096)
            elif name.startswith("wba_"):
                hf = int(name[4:])
                prep_piece(name, w_ba[:, hf * 512:(hf + 1) * 512].rearrange("(h p) n -> p h n", p=64),
                           wbas[hf, :, :], 64, 4096)

    prep_pending = [None]

    def run_prep(n):
        for _ in range(n):
            if prep_pending[0] is not None:
                prep_pending[0]()
                prep_pending[0] = None
            elif prep_tasks:
                ld, fin = prep_tasks.pop(0)
                ld()
                fin()

    def prep_step():
        if prep_pending[0] is not None:
            prep_pending[0]()
            prep_pending[0] = None
        if prep_tasks:
            ld, fin = prep_tasks.pop(0)
            ld()
            prep_pending[0] = fin

    ws_i = [0]

    def load_w(key, src_ap, npart, nelem):
        i = ws_i[0]
        ws_i[0] += 1
        t = gcarve(O_WS[i % 2], 8 * KB, "ws")
        v = t.ap[0:npart, 0:nelem]
        dma(v, src_ap, wT[key], [t])
        return t, v

    ev_i = [0]

    def evac_eng():
        ev_i[0] += 1
        return "act" if ev_i[0] % 2 == 0 else "dve"

    def rstd_from_ss(ss_ap, out_ap, n, width, reads_t, writes_t):
        ts("pool", out_ap, ss_ap, 1.0 / width, ALU.mult, reads_t, writes_t, s2=EPS, op1=ALU.add)
        tt("pool", out_ap, out_ap, smallv[:, C_NH:C_NH + n], ALU.pow, list(writes_t) + [small], writes_t)

    def norm_transpose(xt, gain_t, hT, sscol):
        xv = f32v(xt, "p (t d) -> p t d", t=NT)
        sv = f32v(stats)
        hb = gcarve(O_HB, 8 * KB, "hb")
        hbv = b16v(hb, "p (t d) -> p t d", t=NT)
        hTv = b16v(hT, "p (c n) -> p c n", c=8)
        for t in range(NT):
            act(hbv[:, t, :], xv[:, t, :], AF.Square, [xt], [hb, stats], accum=sv[:, sscol + t:sscol + t + 1])
        rstd_from_ss(sv[:, sscol:sscol + NT], sv[:, sscol + 4:sscol + 4 + NT], NT, float(D), [stats], [stats])
        for t in range(NT):
            stt(hbv[:, t, :], xv[:, t, :], sv[:, sscol + 4 + t:sscol + 5 + t], f32v(gain_t), ALU.mult, ALU.mult,
                [xt, stats, gain_t], [hb])
            pb = PS.next("tr")
            pv = pb.ap.bitcast(BF16)
            for c in range(8):
                tr(pv[:, c * 128:(c + 1) * 128], hbv[:, t, c * 128:(c + 1) * 128], [hb], [pb])
            cp(evac_eng(), hTv[:, :, t * 128:(t + 1) * 128], pv.rearrange("p (c n) -> p c n", c=8), [pb], [hT])

    def load_x(x_ap, tile0, slot):
        xt = gcarve(O_XT if slot == 0 else O_XT2, 16 * KB, "xt")
        dma(f32v(xt, "p (t d) -> p t d", t=NT), x_ap[tile0 * 128:(tile0 + NT) * 128, :].rearrange("(t p) d -> p t d", p=128),
            [], [xt])
        return xt

    tabT = {}
    TCH = 16

    def load_tab(tab_ap, tile0, slot):
        tb = gcarve(O_TAB if slot == 0 else O_TAB2, 3 * KB, "tab")
        key = "p" if tab_ap is tabp else "s"
        dma(f32v(tb, "p (t c) -> p t c", t=NT), tab_ap[tile0:tile0 + NT, :, :].rearrange("t p c -> p t c"),
            [tabT[(key, tile0 // TCH)]], [tb])
        return tb

    def gen_tables():
        TWO_PI = 2.0 * math.pi
        cv = f32v(ctb)
        iot = cv[:, 770:802]
        invr = smallv[:, 96:128]
        inva = smallv[:, 128:144]
        act(invr, iot, AF.Exp, [ctb], [small], scale=-math.log(THETA) / 32.0)
        act(inva, iot[:, 0:16], AF.Exp, [ctb], [small], scale=-math.log(THETA) / 16.0)
        ts("dve", smallv[:, 96:144], smallv[:, 96:144], 1.0 / TWO_PI, ALU.mult, [small], [small])
        W = 96
        GCH = 2 * TCH
        for (key, rc_ap, dst, ntile) in (("p", rcp, tabp, TP // 128), ("s", rcs, tabs, TS // 128)):
            for c0 in range(0, ntile, GCH):
                n = min(GCH, ntile - c0)
                E = n * W
                rc = gcarve(O_P, 256, "rc")
                rcv = f32v(rc)[:, 0:n * 2].rearrange("p (t two) -> p t two", two=2)
                dma(rcv, rc_ap[:, c0:c0 + n, :], [], [rc])
                ang = gcarve(O_P + 1 * KB, 12 * KB, "ang")
                a3 = f32v(ang)[:, 0:E].rearrange("p (t w) -> p t w", w=W)
                for (lo, hi, inv, col) in ((0, 32, invr, 0), (32, 64, invr, 1), (64, 80, inva, 0), (80, 96, inva, 1)):
                    wdt = hi - lo
                    tt("dve", a3[:, :, lo:hi], rcv[:, :, col:col + 1].broadcast_to([128, n, wdt]),
                       inv.unsqueeze(1).broadcast_to([128, n, wdt]), ALU.mult, [rc, small], [ang])
                tbl = gcarve(O_P + 52 * KB, 24 * KB, "tbl")
                t3 = f32v(tbl)[:, 0:n * 192].rearrange("p (t c) -> p t c", c=192)
                for (shift, o_r, o_a) in ((0.25, 0, 128), (0.0, 64, 160)):
                    xs_ = gcarve(O_P + 13 * KB, 12 * KB, "xs")
                    xv_ = f32v(xs_)[:, 0:E]
                    ki = gcarve(O_P + 25 * KB, 12 * KB, "ki")
                    kiv = ki.ap.bitcast(mybir.dt.int32)[:, 0:E]
                    kf = gcarve(O_P + 37 * KB, 12 * KB, "kf")
                    kfv = f32v(kf)[:, 0:E]
                    if shift != 0.0:
                        ts("dve", xv_, f32v(ang)[:, 0:E], shift, ALU.add, [ang], [xs_])
                        src_t, src_v = xs_, xv_
                    else:
                        src_t, src_v = ang, f32v(ang)[:, 0:E]
                    cp("dve", kiv, src_v, [src_t], [ki])
                    cp("dve", kfv, kiv, [ki], [kf])
                    tt("dve", xv_, src_v, kfv, ALU.subtract, [src_t, kf], [xs_])
                    ts("dve", kfv, xv_, 0.5, ALU.is_gt, [xs_], [kf])
                    tt("dve", xv_, xv_, kfv, ALU.subtract, [xs_, kf], [xs_])
                    x3 = xv_.rearrange("p (t w) -> p t w", w=W)
                    act(t3[:, :, o_r:o_r + 64], x3[:, :, 0:64], AF.Sin, [xs_], [tbl], scale=TWO_PI)
                    act(t3[:, :, o_a:o_a + 32], x3[:, :, 64:96], AF.Sin, [xs_], [tbl], scale=TWO_PI)
                tT = T(dst, "tab_%s%d" % (key, c0 // TCH))
                dma(dst[c0:c0 + n, :, :].rearrange("t p c -> p t c"), t3, [tbl], [tT], q="act")
                for j in range(0, n, TCH):
                    tabT[(key, (c0 + j) // TCH)] = tT


    def load_group(x_ap, tab_ap, tile0, slot=0):
        xt = gcarve(O_XT if slot == 0 else O_XT2, 16 * KB, "xt")
        dma(f32v(xt, "p (t d) -> p t d", t=NT), x_ap[tile0 * 128:(tile0 + NT) * 128, :].rearrange("(t p) d -> p t d", p=128),
            [], [xt])
        tb = gcarve(O_TAB if slot == 0 else O_TAB2, 3 * KB, "tab")
        dma(f32v(tb, "p (t c) -> p t c", t=NT), tab_ap[tile0:tile0 + NT, :, :].rearrange("t p c -> p t c"), [], [tb])
        return xt, tb

    def proj_block(hT, wv, width, t, extra=None):
        hTv = b16v(hT, "p (c n) -> p c n", c=8)
        wt, wview = wv
        w3 = wview.rearrange("p (c n) -> p c n", c=8)
        pb = PS.next("proj")
        for c in range(8):
            mm(pb.ap[:, 0:width], hTv[:, c, t * 128:(t + 1) * 128], w3[:, c, :], c == 0, (c == 7 and extra is None),
               [hT, wt], [pb])
        if extra is not None:
            lhsT, rhs, rd = extra
            mm(pb.ap[:, 0:width], lhsT, rhs, False, True, rd, [pb])
        return pb

    quad_i = [0]

    def proj_quad(hT, wv, width, extra=None):
        base = 4 * (quad_i[0] % 2)
        quad_i[0] += 1
        banks = [PS.bank(base + i, "pq") for i in range(4)]
        hTv = b16v(hT, "p (c n) -> p c n", c=8)
        wt, wview = wv
        w3 = wview.rearrange("p (c n) -> p c n", c=8)
        for t in range(NT):
            for c in range(8):
                mm(banks[t].ap[:, 0:width], hTv[:, c, t * 128:(t + 1) * 128], w3[:, c, :], c == 0,
                   (c == 7 and extra is None), [hT, wt], [banks[t]])
            if extra is not None:
                lhsT, rhs, rd_ = extra
                mm(banks[t].ap[:, 0:width], lhsT, rhs, False, True, rd_, [banks[t]])
        return banks, psum_t[:, base:base + 4, :]

    def rope4(eng, src_ts, src4, dst_t, dst4, H, hd2, cos4, sin4, tab_t, t1, t2, nt=NT):
        s5 = src4.rearrange("p t (h i two) -> p t h i two", h=H, two=2)
        d5 = dst4.rearrange("p t (h i two) -> p t h i two", h=H, two=2)
        e_, o_ = s5[:, :, :, :, 0], s5[:, :, :, :, 1]
        c4 = cos4.unsqueeze(2).broadcast_to([128, nt, H, hd2])
        s4 = sin4.unsqueeze(2).broadcast_to([128, nt, H, hd2])
        n = nt * H * hd2
        a4 = f32v(t1)[:, 0:n].rearrange("p (t h i) -> p t h i", t=nt, h=H)
        b4 = f32v(t2)[:, 0:n].rearrange("p (t h i) -> p t h i", t=nt, h=H)
        rs = list(src_ts) + [tab_t]
        tt(eng, a4, e_, c4, ALU.mult, rs, [t1])
        tt(eng, b4, o_, s4, ALU.mult, rs, [t2])
        tt(eng, d5[:, :, :, :, 0], a4, b4, ALU.subtract, [t1, t2], [dst_t])
        tt(eng, a4, e_, s4, ALU.mult, rs, [t1])
        tt(eng, b4, o_, c4, ALU.mult, rs, [t2])
        tt(eng, d5[:, :, :, :, 1], a4, b4, ALU.add, [t1, t2], [dst_t])

    ct_i = [0]

    def ctmp(name="ct"):
        i = ct_i[0]
        ct_i[0] += 1
        return gcarve(O_CT + (i % 8) * 2 * KB, 2 * KB, name)

    def rope(eng, src_t, src_v, dst_t, dst_v, H, hd2, cosv, sinv, tab_t):
        s4 = src_v.rearrange("p (h i two) -> p h i two", h=H, two=2)
        d4 = dst_v.rearrange("p (h i two) -> p h i two", h=H, two=2)
        e_, o_ = s4[:, :, :, 0], s4[:, :, :, 1]
        c3 = cosv.unsqueeze(1).broadcast_to([128, H, hd2])
        s3 = sinv.unsqueeze(1).broadcast_to([128, H, hd2])
        n = H * hd2
        ta, tb_ = ctmp("rp_a"), ctmp("rp_b")
        a3 = f32v(ta)[:, 0:n].rearrange("p (h i) -> p h i", h=H)
        b3 = f32v(tb_)[:, 0:n].rearrange("p (h i) -> p h i", h=H)
        tt(eng, a3, e_, c3, ALU.mult, [src_t, tab_t], [ta])
        tt(eng, b3, o_, s3, ALU.mult, [src_t, tab_t], [tb_])
        tt(eng, d4[:, :, :, 0], a3, b3, ALU.subtract, [ta, tb_], [dst_t])
        tc_, td = ctmp("rp_c"), ctmp("rp_d")
        c3b = f32v(tc_)[:, 0:n].rearrange("p (h i) -> p h i", h=H)
        d3b = f32v(td)[:, 0:n].rearrange("p (h i) -> p h i", h=H)
        tt(eng, c3b, e_, s3, ALU.mult, [src_t, tab_t], [tc_])
        tt(eng, d3b, o_, c3, ALU.mult, [src_t, tab_t], [td])
        tt(eng, d4[:, :, :, 1], c3b, d3b, ALU.add, [tc_, td], [dst_t])

    def head_rms(src_t, src_v, H, gain_t, sscol):
        sv = f32v(stats)
        n = H * DH
        sq = ctmp("sq")
        sqv = f32v(sq)[:, 0:n]
        tt("dve", sqv, src_v, src_v, ALU.mult, [src_t], [sq])
        S.dve(lambda e, o=sv[:, sscol:sscol + H], i=sqv.rearrange("p (h d) -> p h d", h=H):
              e.tensor_reduce(out=o, in_=i, axis=AX.X, op=ALU.add), [sq], [stats])
        rstd_from_ss(sv[:, sscol:sscol + H], sv[:, sscol + 8:sscol + 8 + H], H, float(DH), [stats], [stats])
        s3 = src_v.rearrange("p (h d) -> p h d", h=H)
        tt("dve", s3, s3, sv[:, sscol + 8:sscol + 8 + H].unsqueeze(2).broadcast_to([128, H, DH]), ALU.mult,
           [src_t, stats], [src_t])
        tt("pool", s3, s3, f32v(gain_t).unsqueeze(1).broadcast_to([128, H, DH]), ALU.mult, [src_t, gain_t], [src_t])

    dramT = {}

    def dT(name, idx, ap):
        k = (name, idx)
        if k not in dramT:
            dramT[k] = T(ap, "%s%d" % (name, idx))
        return dramT[k]

    def kva_block(hT, tb, key_tile0, ktT_list, vT_list, wv):
        tbv = f32v(tb, "p (t c) -> p t c", t=NT)
        kst = gcarve(O_E, 1 * KB, "kst")
        vst = gcarve(O_E + 1 * KB, 1056 + 1056, "vst")
        kstv = b16v(kst)
        vstv = b16v(vst)[:, 0:NT * NKV * 66].rearrange("p (t k c) -> p t k c", t=NT, k=NKV)
        memset("pool", vstv[:, :, :, 64:66], 1.0, [vst])
        sv = f32v(stats)
        banks, P4 = proj_quad(hT, wv, 256)
        cp("dve", vstv[:, :, :, 0:64], P4[:, :, 128:256].rearrange("p t (k c) -> p t k c", k=NKV), banks, [vst])
        kg = gcarve(O_E + 5 * KB, 2 * KB, "kg")
        kg3 = f32v(kg)[:, 0:512].rearrange("p (t n) -> p t n", t=NT)
        kg4 = f32v(kg)[:, 0:512].rearrange("p (t k d) -> p t k d", t=NT, k=NKV)
        kc = gcarve(O_E + 7 * KB, 2 * KB, "kc")
        kc3 = f32v(kc)[:, 0:512].rearrange("p (t n) -> p t n", t=NT)
        kc4 = f32v(kc)[:, 0:512].rearrange("p (t k d) -> p t k d", t=NT, k=NKV)
        cp("act", kc3, P4[:, :, 0:128], banks, [kc])
        act(kg3, P4[:, :, 0:128], AF.Square, banks, [kg])
        S.dve(lambda e, o=sv[:, 16:24], i=f32v(kg)[:, 0:512].rearrange("p (a d) -> p a d", d=DH):
              e.tensor_reduce(out=o, in_=i, axis=AX.X, op=ALU.add), [kg], [stats])
        rstd_from_ss(sv[:, 16:24], sv[:, 24:32], 8, float(DH), [stats], [stats])
        tt("dve", kg4, kc4,
           sv[:, 24:32].rearrange("p (t k) -> p t k", t=NT).unsqueeze(3).broadcast_to([128, NT, NKV, DH]), ALU.mult,
           [kc, stats], [kg])
        tt("dve", kg4, kg4, f32v(gk).unsqueeze(1).unsqueeze(1).broadcast_to([128, NT, NKV, DH]), ALU.mult, [kg, gk], [kg])
        kab = gcarve(O_E + 3584, 1 * KB, "kab")
        kab3 = b16v(kab)[:, 0:512].rearrange("p (t n) -> p t n", t=NT)
        r1 = gcarve(O_E + 12 * KB, 2 * KB, "r1")
        r2 = gcarve(O_E + 14 * KB, 2 * KB, "r2")
        rope4("dve", [kg], kg3, kab, kab3, NKV, 32, tbv[:, :, 128:160], tbv[:, :, 160:192], tb, r1, r2)
        p2 = PS.next("trk")
        p2v = p2.ap.bitcast(BF16)
        for t in range(NT):
            tr(p2v[:, t * 128:(t + 1) * 128], kab3[:, t, :], [kab], [p2])
        cp("act", kstv[:, 0:512], p2v[:, 0:512], [p2], [kst])
        ck("k5")
        kt_t = dT("kt", key_tile0 // NT, kts)
        v_t = dT("v", key_tile0 // NT, vs)
        k0 = key_tile0 * 128
        dma(kts[:, k0:k0 + G], kstv, [kst], [kt_t], q="act")
        for k in range(NKV):
            dma(vs[k, :, key_tile0 * 66:(key_tile0 + NT) * 66].rearrange("p (t c) -> p t c", t=NT), vstv[:, :, k, :],
                [vst], [v_t], q="act")
        ktT_list.append(kt_t)
        vT_list.append(v_t)

    def kr_vr_blocks(hT, tb, want_kr_T, wv):
        tbv = f32v(tb, "p (t c) -> p t c", t=NT)
        krb = gcarve(O_KRT if not want_kr_T else O_QBT, 4 * KB, "krb")
        krbv = b16v(krb, "p (t n) -> p t n", t=NT)
        banks, P4 = proj_quad(hT, wv, 512)
        wv0 = load_w("vr0", wA[WIN_IDX["vr0"], :, :], 128, 4096)
        krc = gcarve(O_E + 16 * KB, 8 * KB, "krc")
        krc3 = f32v(krc, "p (t n) -> p t n", t=NT)
        cp("act", krc3, P4, banks, [krc])
        r1 = gcarve(O_E + 12 * KB, 2 * KB, "r1")
        r2 = gcarve(O_E + 14 * KB, 2 * KB, "r2")
        for t0_ in (0, 2):
            rope4("dve", [krc], krc3[:, t0_:t0_ + 2, :], krb, krbv[:, t0_:t0_ + 2, :], NH_R, 64,
                  tbv[:, t0_:t0_ + 2, 0:64], tbv[:, t0_:t0_ + 2, 64:128], tb, r1, r2, nt=2)
        vr = gcarve(O_VR, 8 * KB, "vr")
        vrv = b16v(vr, "p (t n) -> p t n", t=NT)
        banks, P4 = proj_quad(hT, wv0, 512)
        wv1 = load_w("vr1", wA[WIN_IDX["vr1"], :, :], 128, 4096)
        cp("act", vrv[:, :, 0:512], P4, banks, [vr])
        banks, P4 = proj_quad(hT, wv1, 512)
        cp("act", vrv[:, :, 512:1024], P4, banks, [vr])
        return krb, vr

    def others_gen(og, ktT_list, vT_list, xts):
        tile0 = (SEG // 128) + og * NT
        xt = xts[og]
        tb = load_tab(tabs, tile0, og % 2)
        dt_ = gcarve(O_DST + (og % 2) * 64, 64, "dst")
        dtv = f32v(dt_)[:, 0:NT * 2].rearrange("p (t two) -> p t two", two=2)
        dma(dtv, dists[:, tile0:tile0 + NT, :], [], [dt_])
        hT = gcarve(O_HT if og % 2 == 0 else O_HT2, 8 * KB, "hT")
        wkva = load_w("kva", wA[WIN_IDX["kva"], :, 0:8 * 256], 128, 8 * 256)
        wkr = load_w("kr", wA[WIN_IDX["kr"], :, :], 128, 4096)
        norm_transpose(xt, g_mix, hT, 0)
        yield
        kva_block(hT, tb, tile0, ktT_list, vT_list, wkva)
        krb, vr = kr_vr_blocks(hT, tb, False, wkr)
        yield
        krv = b16v(krb, "p (t h d) -> p t h d", t=NT, h=NH_R)
        vrv = b16v(vr, "p (t n) -> p t n", t=NT)
        sv = f32v(stats)
        for t in range(NT):
            act(sv[:, 32 + t * 4:36 + t * 4], smallv[:, C_LGF:C_LGF + 4], AF.Exp, [small, dt_], [stats],
                scale=dtv[:, t, 0:1])
            act(sv[:, 48 + t * 4:52 + t * 4], smallv[:, C_LGB:C_LGB + 4], AF.Exp, [small, dt_], [stats],
                scale=dtv[:, t, 1:2])
        kf = gcarve(O_KF, 4 * KB, "Kfo")
        kb_ = gcarve(O_KB2, 4 * KB, "Kbo")
        kfv = b16v(kf, "p (t h d) -> p t h d", t=NT, h=NH_R)
        kbv = b16v(kb_, "p (t h d) -> p t h d", t=NT, h=NH_R)
        tt("dve", kfv, krv, sv[:, 32:48].rearrange("p (t h) -> p t h", t=NT).unsqueeze(3).broadcast_to([128, NT, NH_R, DK]),
           ALU.mult, [krb, stats], [kf])
        tt("dve", kbv, krv, sv[:, 48:64].rearrange("p (t h) -> p t h", t=NT).unsqueeze(3).broadcast_to([128, NT, NH_R, DK]),
           ALU.mult, [krb, stats], [kb_])
        Fv = f32v(Ff, "p (h e) -> p h e", h=NH_R)
        Bv = f32v(Bf, "p (h e) -> p h e", h=NH_R)
        for (kx, kxv, acc_t, accv) in ((kf, kfv, Ff, Fv), (kb_, kbv, Bf, Bv)):
            for hp in range(2):
                pb = PS.next("kvo")
                for hh in range(2):
                    h = hp * 2 + hh
                    for t in range(NT):
                        mm(pb.ap[:, hh * 256:(hh + 1) * 256], kxv[:, t, h, :], vrv[:, t, h * DV:(h + 1) * DV],
                           t == 0, t == NT - 1, [kx, vr], [pb])
                tt("dve", accv[:, hp * 2:hp * 2 + 2, :], accv[:, hp * 2:hp * 2 + 2, :],
                   pb.ap.rearrange("p (h e) -> p h e", h=2), ALU.add, [acc_t, pb], [acc_t])

    def pipeline(gens, n, after_a, after_group):
        next(gens[0])
        after_a(0)
        for k in range(n):
            next(gens[k])
            if k + 1 < n:
                next(gens[k + 1])
                after_a(k + 1)
            next(gens[k], None)
            after_group(k)

    def pass1_gen(x_ap, tab_ap, g, k_idx, key_tile_base, ktT_list, vT_list, bnT, bpar, xts):
        tile0 = g * NT
        xt = xts[k_idx]
        tb = load_tab(tab_ap, tile0, k_idx % 2)
        hT = gcarve(O_HT if k_idx % 2 == 0 else O_HT2, 8 * KB, "hT")
        wkva = load_w("kva", wA[WIN_IDX["kva"], :, 0:8 * 256], 128, 8 * 256)
        wkr = load_w("kr", wA[WIN_IDX["kr"], :, :], 128, 4096)
        ck("p1x")
        norm_transpose(xt, g_mix, hT, 0)
        yield
        kva_block(hT, tb, key_tile_base + tile0, ktT_list, vT_list, wkva)
        krb, vr = kr_vr_blocks(hT, tb, False, wkr)
        yield
        krv = b16v(krb, "p (t h d) -> p t h d", t=NT, h=NH_R)
        vrv = b16v(vr, "p (t n) -> p t n", t=NT)
        kb_ = gcarve(O_KB2, 4 * KB, "Kb")
        kbv = b16v(kb_, "p (t h d) -> p t h d", t=NT, h=NH_R)
        tt("dve", kbv, krv, smallv[:, C_KDB:C_KDB + 4].unsqueeze(1).unsqueeze(3).broadcast_to([128, NT, NH_R, DK]),
           ALU.mult, [krb, small], [kb_])
        Bv = f32v(Bf, "p (h e) -> p h e", h=NH_R)
        for t in reversed(range(NT)):
            n = tile0 + t
            cur = Bbf[bpar[0] % 2]
            bt = dT("bn", n, bns)
            dma(bns[n, :, :], cur[:, :], [cur], [bt], q="act")
            bnT[n] = bt
            for hp in range(2):
                pb = PS.next("kvb")
                for hh in range(2):
                    h = hp * 2 + hh
                    mm(pb.ap[:, hh * 256:(hh + 1) * 256], kbv[:, t, h, :], vrv[:, t, h * DV:(h + 1) * DV], True, True,
                       [kb_, vr], [pb])
                for hh in range(2):
                    h = hp * 2 + hh
                    stt(Bv[:, h, :], Bv[:, h, :], smallv[:, C_CDB + h:C_CDB + h + 1], pb.ap[:, hh * 256:(hh + 1) * 256],
                        ALU.mult, ALU.add, [Bf, small, pb], [Bf])
            bpar[0] += 1
            nxt = Bbf[bpar[0] % 2]
            cp("act", nxt[:, :], f32v(Bf), [Bf], [nxt])

    def pass2_group(x_ap, tab_ap, out_ap, g, nkeys, ktT_list, vT_list, bnT, fpar):
        tile0 = g * NT
        xt, tb = load_group(x_ap, tab_ap, tile0)
        tbv = f32v(tb, "p (t c) -> p t c", t=NT)
        xv = f32v(xt, "p (t d) -> p t d", t=NT)
        bng = gcarve(O_BN, 8 * KB, "bng")
        bngv = b16v(bng, "p (t n) -> p t n", t=NT)
        dma(bngv, bns[tile0:tile0 + NT, :, :].rearrange("t p n -> p t n"), [bnT[tile0 + t] for t in range(NT)], [bng])
        hT = gcarve(O_HT, 8 * KB, "hT")
        norm_transpose(xt, g_mix, hT, 0)
        sv = f32v(stats)

        QT = gcarve(O_QT, 8 * KB, "QT")
        QTv = b16v(QT, "p (h n) -> p h n", h=NH_A)
        memset("pool", QTv[0:64, 4:8, :], 0.0, [QT])
        memset("pool", QTv[64:128, 0:4, :], 0.0, [QT])
        tb4 = tbv
        r1 = gcarve(O_E + 12 * KB, 4 * KB, "r1")
        r2 = gcarve(O_E + 16 * KB, 4 * KB, "r2")
        big = gcarve(O_E + 4 * KB, 8 * KB, "cbig")
        big3 = f32v(big, "p (t n) -> p t n", t=NT)
        big4 = f32v(big, "p (t h d) -> p t h d", t=NT, h=NH_A)
        rb = gcarve(O_E + 20 * KB, 4 * KB, "rb")
        rb3 = b16v(rb, "p (t n) -> p t n", t=NT)

        def tr_pairs(src3, reads, bank_ids):
            outs = []
            for tp_ in range(2):
                p2 = PS.bank(bank_ids[tp_], "trc")
                p2v = p2.ap.bitcast(BF16)
                for tl in range(2):
                    t = tp_ * 2 + tl
                    for j in range(4):
                        tr(p2v[:, (tl * 4 + j) * 128:(tl * 4 + j + 1) * 128], src3[:, t, j * 128:(j + 1) * 128], reads, [p2])
                outs.append((p2, p2v.rearrange("p (t h n) -> p h t n", t=2, h=4)))
            return outs

        wv = load_w("qa", wA[WIN_IDX["qa"], :, :], 128, 4096)
        banks_qa, P4_qa = proj_quad(hT, wv, 512)
        act(big3, P4_qa, AF.Square, banks_qa, [big])
        S.dve(lambda e, o=sv[:, 128:160], i=f32v(big).rearrange("p (a d) -> p a d", d=DH):
              e.tensor_reduce(out=o, in_=i, axis=AX.X, op=ALU.add), [big], [stats])
        rstd_from_ss(sv[:, 128:160], sv[:, 160:192], 32, float(DH), [stats], [stats])
        qrT = gcarve(O_QRT, 4 * KB, "qrT")
        QfT = gcarve(O_QFT, 4 * KB, "QfT")
        QbT = gcarve(O_QBT, 4 * KB, "QbT")
        qrTv = b16v(qrT, "p (h n) -> p h n", h=NH_R)
        QfTv = b16v(QfT, "p (h n) -> p h n", h=NH_R)
        QbTv = b16v(QbT, "p (h n) -> p h n", h=NH_R)
        wv = load_w("qr", wA[WIN_IDX["qr"], :, :], 128, 4096)
        rq = gcarve(O_E, 4 * KB, "rq")
        rq3 = b16v(rq, "p (t n) -> p t n", t=NT)
        banks, P4 = proj_quad(hT, wv, 512)
        rope4("dve", banks, P4, rq, rq3, NH_R, 64, tb4[:, :, 0:64], tb4[:, :, 64:128], tb, r1, r2)
        qdf = f32v(QDf, "p (h i) -> p h i", h=4).unsqueeze(2).broadcast_to([128, 4, 2, 128])
        qdb = f32v(QDb, "p (h i) -> p h i", h=4).unsqueeze(2).broadcast_to([128, 4, 2, 128])
        for tp_, (p2, pv4) in enumerate(tr_pairs(rq3, [rq], (banks[0].idx, banks[1].idx))):
            tsl = slice(tp_ * 256, (tp_ + 1) * 256)
            act(qrTv[:, :, tsl].rearrange("p h (t n) -> p h t n", t=2), pv4, AF.Copy, [p2], [qrT], scale=float(DK) ** -0.5)
            tt("dve", QfTv[:, :, tsl].rearrange("p h (t n) -> p h t n", t=2), pv4, qdf, ALU.mult, [p2, QDf], [QfT])
            tt("dve", QbTv[:, :, tsl].rearrange("p h (t n) -> p h t n", t=2), pv4, qdb, ALU.mult, [p2, QDb], [QbT])
        tt("dve", big4, P4_qa.rearrange("p t (h d) -> p t h d", h=NH_A),
           sv[:, 160:192].rearrange("p (t h) -> p t h", t=NT).unsqueeze(3).broadcast_to([128, NT, NH_A, DH]), ALU.mult,
           list(banks_qa) + [stats], [big])
        tt("dve", big4, big4, f32v(gq).unsqueeze(1).unsqueeze(1).broadcast_to([128, NT, NH_A, DH]), ALU.mult, [big, gq], [big])
        rope4("dve", [big], big3, rb, rb3, NH_A, 32, tb4[:, :, 128:160], tb4[:, :, 160:192], tb, r1, r2)
        for tp_, (p2, pv4) in enumerate(tr_pairs(rb3, [rb], (banks_qa[0].idx, banks_qa[1].idx))):
            tsl = slice(tp_ * 256, (tp_ + 1) * 256)
            cp("act", QTv[0:64, 0:4, tsl].rearrange("p h (t n) -> p h t n", t=2), pv4[0:64], [p2], [QT])
            cp("dve", QTv[64:128, 4:8, tsl].rearrange("p h (t n) -> p h t n", t=2), pv4[64:128], [p2], [QT])
        krb = gcarve(O_E, 4 * KB, "krb2")
        krbv = b16v(krb, "p (t n) -> p t n", t=NT)
        krT = gcarve(O_KRT, 4 * KB, "krT")
        krTv = b16v(krT, "p (h n) -> p h n", h=NH_R)
        Kf = gcarve(O_KF, 4 * KB, "Kf")
        Kfv = b16v(Kf, "p (t h d) -> p t h d", t=NT, h=NH_R)
        wv = load_w("kr", wA[WIN_IDX["kr"], :, :], 128, 4096)
        banks, P4 = proj_quad(hT, wv, 512)
        rope4("dve", banks, P4, krb, krbv, NH_R, 64, tb4[:, :, 0:64], tb4[:, :, 64:128], tb, r1, r2)
        for tp_, (p2, pv4) in enumerate(tr_pairs(krbv, [krb], (banks[0].idx, banks[1].idx))):
            tsl = slice(tp_ * 256, (tp_ + 1) * 256)
            cp("act", krTv[:, :, tsl].rearrange("p h (t n) -> p h t n", t=2), pv4, [p2], [krT])
        tt("pool", Kfv, b16v(krb, "p (t h d) -> p t h d", t=NT, h=NH_R),
           smallv[:, C_KDF:C_KDF + 4].unsqueeze(1).unsqueeze(3).broadcast_to([128, NT, NH_R, DK]), ALU.mult, [krb, small], [Kf])
        vr = gcarve(O_VR, 8 * KB, "vr")
        vrv = b16v(vr, "p (t n) -> p t n", t=NT)
        for half in range(2):
            wv = load_w("vr%d" % half, wA[WIN_IDX["vr%d" % half], :, :], 128, 4096)
            banks, P4 = proj_quad(hT, wv, 512)
            cp("act", vrv[:, :, half * 512:(half + 1) * 512], P4, banks, [vr])
        Fv = f32v(Ff, "p (h e) -> p h e", h=NH_R)
        Fb4 = gcarve(O_HB, 8 * KB, "Fb4")
        Fb4v = b16v(Fb4, "p (t h e) -> p t h e", t=NT, h=NH_R)
        for t in range(NT):
            cp("act", Fb4v[:, t], Fv, [Ff], [Fb4])
            for hp in range(2):
                pk = PS.next("kvf")
                for hh in range(2):
                    h = hp * 2 + hh
                    mm(pk.ap[:, hh * 256:(hh + 1) * 256], Kfv[:, t, h, :], vrv[:, t, h * DV:(h + 1) * DV], True, True,
                       [Kf, vr], [pk])
                for hh in range(2):
                    h = hp * 2 + hh
                    stt(Fv[:, h, :], Fv[:, h, :], smallv[:, C_CDF + h:C_CDF + h + 1], pk.ap[:, hh * 256:(hh + 1) * 256],
                        ALU.mult, ALU.add, [Ff, small, pk], [Ff])
        gsil = gcarve(O_GSIL, 8 * KB, "gsil")
        gsv = b16v(gsil, "p (t n) -> p t n", t=NT)
        for half in range(2):
            wv = load_w("gr%d" % half, wA[WIN_IDX["gr%d" % half], :, :], 128, 4096)
            banks, P4 = proj_quad(hT, wv, 512)
            th = gcarve(O_E + 4 * KB + half * 0, 8 * KB, "gth")
            th3 = f32v(th, "p (t n) -> p t n", t=NT)
            act(th3, P4, AF.Tanh, banks, [th], scale=0.5)
            stt(th3, th3, 1.0, P4, ALU.add, ALU.mult, [th] + list(banks), [th])
            tt("pool", gsv[:, :, half * 512:(half + 1) * 512], th3,
               f32v(g_ret)[:, half * 512:(half + 1) * 512].unsqueeze(1).broadcast_to([128, NT, 512]), ALU.mult,
               [th, g_ret], [gsil])

        ck("C")
        retT = gcarve(O_RETT, 8 * KB, "retT")
        retTv = b16v(retT, "p (c n) -> p c n", c=8)
        mskv = f32v(maskT, "p (h i) -> p h i", h=4)
        ryn = {}

        def ret_a(t):
            ts_ = slice(t * 128, (t + 1) * 128)
            pa = PS.bank(5, "AT")
            for h in range(NH_R):
                mm(pa.ap[:, h * 128:(h + 1) * 128], krTv[:, h, ts_], qrTv[:, h, ts_], True, True, [krT, qrT], [pa])
            AM = gcarve(O_WS[0] + (t % 2) * KB, KB, "AM")
            AMv = b16v(AM, "p (h i) -> p h i", h=NH_R)
            tt("dve", AMv, pa.ap.rearrange("p (h i) -> p h i", h=NH_R), mskv, ALU.mult, [pa, maskT], [AM])
            ybanks = [PS.bank(7, "y01"), PS.bank(5, "y23")]
            for hp in range(2):
                py = ybanks[hp]
                for hh in range(2):
                    h = hp * 2 + hh
                    o_ = py.ap[:, hh * 256:(hh + 1) * 256]
                    mm(o_, AMv[:, h, :], vrv[:, t, h * DV:(h + 1) * DV], True, False, [AM, vr], [py])
                    mm(o_, QfTv[:, h, ts_], Fb4v[:, t, h, :], False, False, [QfT, Fb4], [py])
                    mm(o_, QbTv[:, h, ts_], bngv[:, t, h * DV:(h + 1) * DV], False, True, [QbT, bng], [py])
            sF = statsF[t % 2]
            sFv = f32v(sF)
            yn = gcarve(O_WS[0] + 2 * KB + (t % 2) * 2 * KB, 2 * KB, "yn")
            ynv = b16v(yn)
            for h in range(NH_R):
                py = ybanks[h // 2]
                S.dve(lambda e, o=sFv[:, h * 6:h * 6 + 6], i_=py.ap[:, (h % 2) * 256:(h % 2 + 1) * 256]:
                      e.bn_stats(out=o, in_=i_), [py], [sF])
                S.dve(lambda e, o=sFv[:, 32 + h * 2:34 + h * 2], i_=sFv[:, h * 6:h * 6 + 6]:
                      e.bn_aggr(out=o, in_=i_), [sF], [sF])
            mvv = sFv[:, 32:40].rearrange("p (h two) -> p h two", two=2)
            ts("pool", sFv[:, 40:44], mvv[:, :, 1], EPS, ALU.add, [sF], [sF])
            tt("pool", sFv[:, 40:44], sFv[:, 40:44], smallv[:, C_NH:C_NH + 4], ALU.pow, [sF, small], [sF])
            for h in range(NH_R):
                py = ybanks[h // 2]
                ts("dve", ynv[:, h * DV:(h + 1) * DV], py.ap[:, (h % 2) * 256:(h % 2 + 1) * 256],
                   sFv[:, 32 + 2 * h:33 + 2 * h], ALU.subtract, [py, sF], [yn], s2=sFv[:, 40 + h:41 + h], op1=ALU.mult)
            ryn[t] = yn

        def ret_b(t):
            ts_ = slice(t * 128, (t + 1) * 128)
            yn = ryn[t]
            rt = gcarve(O_WS[1] + (t % 2) * 2 * KB, 2 * KB, "ret")
            tt("dve", rt[:, :], b16v(yn), gsv[:, t, :], ALU.mult, [yn, gsil], [rt])
            p2 = PS.bank(7, "trr")
            p2v = p2.ap.bitcast(BF16)
            for c in range(8):
                tr(p2v[:, c * 128:(c + 1) * 128], rt[:, c * 128:(c + 1) * 128], [rt], [p2])
            cp("dve", retTv[:, :, ts_], p2v.rearrange("p (c n) -> p c n", c=8), [p2], [retT])

        def ret_step(k):
            if 1 <= k <= NT:
                ret_b(k - 1)
            if k < NT:
                ret_a(k)

        attT = gcarve(O_ATT, 8 * KB, "attT")
        attv = b16v(attT, "p (h n) -> p h n", h=NH_A)
        memset("pool", attv[64:128, :, :], 0.0, [attT])
        rd = gcarve(O_E + 21 * KB, 2 * KB, "rd")
        rdv = f32v(rd)
        memset("pool", rdv, 0.0, [rd])
        KBLK = min(2048, nkeys)
        nblk = nkeys // KBLK
        npair = KBLK // 256
        e_i = [0]
        nbias = smallv[:, C_NB:C_NB + 1]
        deferred = [None]
        for h in range(NH_A):
            kv = h // (NH_A // NKV)
            Ob = PS.bank(6, "O")
            pend = None
            cnt = 0
            total = nblk * npair
            for blk in range(nblk):
                i = e_i[0]
                e_i[0] += 1
                ktb = gcarve(O_E + (i % 2) * 4 * KB, 4 * KB, "ktb")
                vb = gcarve(O_E + 8 * KB + (i % 2) * 2112, 2112, "vb")
                ktbv = b16v(ktb)[:, 0:KBLK]
                vbv = b16v(vb)[:, 0:(KBLK // 128) * 66].rearrange("p (t c) -> p t c", c=66)
                g0, g1 = blk * KBLK // G, (blk + 1) * KBLK // G
                dma(ktbv, kts[:, blk * KBLK:(blk + 1) * KBLK], ktT_list[g0:g1], [ktb])
                dma(vbv, vs[kv, :, (blk * KBLK // 128) * 66:((blk + 1) * KBLK // 128) * 66].rearrange("p (t c) -> p t c", c=66),
                    vT_list[g0:g1], [vb])
                for pr in range(npair):
                    b0 = 2 * (cnt % 2)
                    s0, s1 = PS.bank(b0, "S0"), PS.bank(b0 + 1, "S1")
                    for u, sb_ in enumerate((s0, s1)):
                        kt_ = pr * 2 + u
                        mm(sb_.ap[:, :], ktbv[:, kt_ * 128:(kt_ + 1) * 128], QTv[:, h, :], True, True, [ktb, QT], [sb_])
                    if pend is not None:
                        pend()
                    if deferred[0] is not None and cnt == 2:
                        deferred[0]()
                        deferred[0] = None
                    Pt = gcarve(O_E + 13 * KB + (cnt % 2) * 2 * KB, 2 * KB, "P")
                    Pv = b16v(Pt, "p (u n) -> p u n", u=2)
                    act(Pv, psum_t[:, b0:b0 + 2, :], AF.Exp, [s0, s1, small], [Pt], scale=float(DH) ** -0.5, bias=nbias)

                    def pv_step(Pt=Pt, Pv=Pv, vb=vb, vbv=vbv, pr=pr, first=(cnt == 0), last=(cnt == total - 1), Ob=Ob):
                        for u in range(2):
                            mm(Ob.ap[0:65, :], vbv[:, pr * 2 + u, 0:65], Pv[:, u, :], first and u == 0, last and u == 1,
                               [vb, Pt], [Ob])
                    pend = pv_step
                    cnt += 1
            pend()
            if deferred[0] is not None:
                deferred[0]()
                deferred[0] = None
            Ou = gcarve(O_E + 17 * KB + (h % 2) * 2 * KB, 2 * KB, "Ou")
            Ouv = f32v(Ou)
            cp("act", Ouv[0:65, :], Ob.ap[0:65, :], [Ob], [Ou])
            S.dve(lambda e, o=rdv[64:65, :], i_=Ouv[64:65, :]: e.reciprocal(out=o, in_=i_), [Ou], [rd])

            def fin(h=h, Ou=Ou, Ouv=Ouv):
                bc = PS.bank(4, "bc")
                mm(bc.ap[0:64, :], f32v(ones_f)[:, 0:64], rdv[:, :], True, True, [ones_f, rd], [bc])
                tt("dve", attv[0:64, h, :], Ouv[0:64, :], bc.ap[0:64, :], ALU.mult, [Ou, bc], [attT])
            deferred[0] = fin
            ret_step(h)

        if deferred[0] is not None:
            deferred[0]()
            deferred[0] = None
        ck("E")
        ck("F")
        tg = gcarve(O_TG, 16 * KB, "tg")
        tgv = b16v(tg, "p (t n) -> p t n", t=NT)
        for blk in range(4):
            wv = load_w("gt%d" % blk, wA[WIN_IDX["gt%d" % blk], :, :], 128, 4096)
            for t in range(NT):
                pb = proj_block(hT, wv, 512, t,
                                extra=(ones_b[:, 0:128], bgb[:, blk * 512:(blk + 1) * 512], [ones_b, bgb]))
                act(tgv[:, t, blk * 512:(blk + 1) * 512], pb.ap[:, :], AF.Tanh, [pb], [tg], scale=0.5)
        mix = gcarve(O_MIX, 8 * KB, "mix")
        mixv = b16v(mix, "p (t n) -> p t n", t=NT)
        for half in range(2):
            wbr_t, wbr_v = load_w("wbr_%d" % half, wbrs[half, :, :], 128, 4096)
            wba_t, wba_v = load_w("wba_%d" % half, wbas[half, :, :], 64, 4096)
            wba_v = wba_t.ap[:, 0:4096]
            wbr3 = wbr_v.rearrange("p (c n) -> p c n", c=8)
            wba3 = wba_v.rearrange("p (c n) -> p c n", c=8)
            hs = slice(half * 512, (half + 1) * 512)
            for t in range(NT):
                ts_ = slice(t * 128, (t + 1) * 128)
                pr_ = PS.next("br")
                for c in range(8):
                    mm(pr_.ap[:, :], retTv[:, c, ts_], wbr3[:, c, :], c == 0, c == 7, [retT, wbr_t], [pr_])
                pa_ = PS.next("ba")
                for h in range(NH_A):
                    mm(pa_.ap[:, :], attv[:, h, ts_], wba3[:, h, :], h == 0, h == NH_A - 1, [attT, wba_t], [pa_])
                m1 = gcarve(O_M1 + (t % 2) * 2 * KB, 2 * KB, "m1")
                stt(f32v(m1), tgv[:, t, half * 512:(half + 1) * 512], 1.0, pa_.ap[:, :], ALU.add, ALU.mult, [tg, pa_], [m1])
                m2 = gcarve(O_M1 + 4 * KB + (t % 2) * 2 * KB, 2 * KB, "m2")
                stt(f32v(m2), tgv[:, t, 1024 + half * 512:1024 + (half + 1) * 512], 1.0, pr_.ap[:, :], ALU.add, ALU.mult,
                    [tg, pr_], [m2])
                tt("pool", mixv[:, t, hs], f32v(m1), f32v(m2), ALU.add, [m1, m2], [mix])
        mixT = gcarve(O_MIXT, 8 * KB, "mixT")
        mixTv = b16v(mixT, "p (c n) -> p c n", c=8)
        for t in range(NT):
            p2 = PS.next("trm")
            p2v = p2.ap.bitcast(BF16)
            for c in range(8):
                tr(p2v[:, c * 128:(c + 1) * 128], mixv[:, t, c * 128:(c + 1) * 128], [mix], [p2])
            cp(evac_eng(), mixTv[:, :, t * 128:(t + 1) * 128], p2v.rearrange("p (c n) -> p c n", c=8), [p2], [mixT])
        for half in range(2):
            wo_t, wo_v = load_w("wout_%d" % half, wouts[half, :, :], 128, 4096)
            wo3 = wo_v.rearrange("p (c n) -> p c n", c=8)
            for t in range(NT):
                po = PS.next("wo")
                for c in range(8):
                    mm(po.ap[:, :], mixTv[:, c, t * 128:(t + 1) * 128], wo3[:, c, :], c == 0, c == 7, [mixT, wo_t], [po])
                stt(xv[:, t, half * 512:(half + 1) * 512], po.ap[:, :], 0.5, xv[:, t, half * 512:(half + 1) * 512],
                    ALU.mult, ALU.add, [po, xt], [xt])
        ck("G")
        h2T = gcarve(O_HT, 8 * KB, "h2T")
        norm_transpose(xt, g_ffn, h2T, 8)
        h2v = b16v(h2T, "p (c n) -> p c n", c=8)
        hid = gcarve(O_HID, 22 * KB, "hid")
        hidv = b16v(hid, "p (f n) -> p f n", f=NFC)
        ft_i = 0
        for fb in range(11):
            w1_t, w1_v = load_w("w1_%d" % fb, w1s[fb, :, :], 128, 4096)
            w1v = w1_v.rearrange("p (u c n) -> p u c n", u=2, c=8)
            for fc in range(2):
                f = fb * 2 + fc
                pg = PS.next("fg")
                pu = PS.next("fu")
                for c in range(8):
                    mm(pg.ap[:, :], w1v[:, 0, c, fc * 128:(fc + 1) * 128], h2v[:, c, :], c == 0, c == 7, [w1_t, h2T], [pg])
                for c in range(8):
                    mm(pu.ap[:, :], w1v[:, 1, c, fc * 128:(fc + 1) * 128], h2v[:, c, :], c == 0, c == 7, [w1_t, h2T], [pu])
                th = gcarve(O_FT + (ft_i % 2) * 2 * KB, 2 * KB, "fth")
                a_ = gcarve(O_FT + 4 * KB + (ft_i % 2) * 2 * KB, 2 * KB, "fa")
                ft_i += 1
                act(f32v(th), pg.ap[:, :], AF.Tanh, [pg], [th], scale=0.5)
                stt(f32v(a_), f32v(th), 1.0, pg.ap[:, :], ALU.add, ALU.mult, [th, pg], [a_])
                tt("dve", hidv[:, f, :], f32v(a_), pu.ap[:, :], ALU.mult, [a_, pu], [hid])
        ck("I")
        for half in range(2):
            w2t = gcarve(O_W2[half], 22 * KB, "w2")
            w2v = b16v(w2t, "p (f n) -> p f n", f=NFC)
            dma(b16v(w2t), w2s[half, :, :], wT["w2_%d" % half], [w2t])
            for t in range(NT):
                po = PS.next("f2")
                for f in range(NFC):
                    mm(po.ap[:, :], hidv[:, f, t * 128:(t + 1) * 128], w2v[:, f, :], f == 0, f == NFC - 1, [hid, w2t], [po])
                stt(xv[:, t, half * 512:(half + 1) * 512], po.ap[:, :], 0.5, xv[:, t, half * 512:(half + 1) * 512],
                    ALU.mult, ALU.add, [po, xt], [xt])
        ots = []
        for t in range(NT):
            ot = gcarve(O_OUT[t % 2] if t < 2 else O_FT + (t - 2) * 4 * KB, 4 * KB, "ot")
            ots.append(ot)
            act(ot.ap[:, 0:1024], xv[:, t, :], AF.Square, [xt], [ot, stats], accum=sv[:, 16 + t:17 + t])
        rstd_from_ss(sv[:, 16:16 + NT], sv[:, 20:20 + NT], NT, float(D), [stats], [stats])
        for t in range(NT):
            ot = ots[t]
            stt(f32v(ot), xv[:, t, :], sv[:, 20 + t:21 + t], f32v(g_fin), ALU.mult, ALU.mult, [xt, stats, g_fin], [ot])
            dma(out_ap[(tile0 + t) * 128:(tile0 + t + 1) * 128, :], f32v(ot), [ot], [], q="pool")

    def run_pass1(x_ap, tab_ap, ng, k_list, v_list, bnT, bpar):
        order = list(reversed(range(ng)))
        xts = {}
        for k in range(min(2, ng)):
            xts[k] = load_x(x_ap, order[k] * NT, k % 2)
        gens = [pass1_gen(x_ap, tab_ap, order[k], k, 0, k_list, v_list, bnT, bpar, xts) for k in range(ng)]

        def after_a(k):
            if k + 2 < ng:
                xts[k + 2] = load_x(x_ap, order[k + 2] * NT, k % 2)
        pipeline(gens, ng, after_a, lambda k: None)

    def run_sequence(x_ap, tab_ap, out_ap, ntok, is_sample):
        ng = ntok // G
        ktT_list, vT_list, bnT = [], [], {}
        bpar, fpar = [0], [0]
        if is_sample:
            memset("pool", f32v(Ff), 0.0, [Ff])
            memset("pool", f32v(Bf), 0.0, [Bf])
            nog = (TS - SEG) // G
            own_k, own_v = [], []
            oth_k, oth_v = [], []
            xts = {}
            t0o = SEG // 128
            for og in range(min(2, nog)):
                xts[og] = load_x(xs, t0o + og * NT, og % 2)
            gens = [others_gen(og, oth_k, oth_v, xts) for og in range(nog)]

            def after_a(og):
                if og + 2 < nog:
                    xts[og + 2] = load_x(xs, t0o + (og + 2) * NT, og % 2)
                if og % 2 == 1:
                    prep_step()
            pipeline(gens, nog, after_a, lambda og: prep_step())
            run_prep(len(prep_tasks) + 1)
            cp("act", Bbf[0][:, :], f32v(Bf), [Bf], [Bbf[0]])
            run_pass1(x_ap, tab_ap, ng, own_k, own_v, bnT, bpar)
            own_k.reverse()
            own_v.reverse()
            ktT_list = own_k + oth_k
            vT_list = own_v + oth_v
            nkeys = TS
        else:
            memset("pool", f32v(Ff), 0.0, [Ff])
            memset("pool", f32v(Bf), 0.0, [Bf])
            memset("pool", Bbf[0][:, :], 0.0, [Bbf[0]])
            run_prep(len(prep_tasks) + 1)
            ck("prepall")
            run_pass1(x_ap, tab_ap, ng, ktT_list, vT_list, bnT, bpar)
            ck("pass1")
            ktT_list.reverse()
            vT_list.reverse()
            nkeys = ntok
        cp("act", Fbf[0][:, :], f32v(Ff), [Ff], [Fbf[0]])
        for g in range(ng):
            pass2_group(x_ap, tab_ap, out_ap, g, nkeys, ktT_list, vT_list, bnT, fpar)

    try:
      setup()
      gen_tables()
      ck("setup")
      make_prep(["kva", "kr", "vr0", "vr1", "qa", "qr", "gr0", "gr1", "gt0", "gt1", "gt2", "gt3",
               "wbr_0", "wba_0", "wbr_1", "wba_1", "wout_0", "wout_1"] +
                ["w1_%d" % i for i in range(11)] + ["w2_0", "w2_1"])
      run_prep(4)
      ck("prep4")
      if cfg.get("do_sample", True):
        run_sequence(xs, tabs, ys, SEG, True)
      for i in range(NP):
        run_sequence(xp[i], tabp, yp[i], TP, False)
    except StopBuild:
      pass

    with nc.Block() as block:
        @block.sync
        def _(sync):
            S.emit(st)
    st.close()
    return nc, S


def rowcol(pos):
    pos = np.asarray(pos)
    rc = np.stack([pos // GW, pos % GW], axis=-1).astype(np.float32)
    return rc.reshape(-1, 128, 2).transpose(1, 0, 2)


def const_table():
    j = np.arange(128, dtype=np.float32)[:, None]
    i = np.arange(128, dtype=np.float32)[None, :]
    c = np.zeros((128, 802), np.float32)
    c[:, 770:802] = np.arange(32, dtype=np.float32)[None, :]
    c[:, 0:128] = np.maximum(i - j, 0)
    c[:, 128:256] = np.maximum(j - i, 0)
    c[:, 256:384] = (i >= j)
    c[:, 384:512] = (i < j)
    c[:, 512:640] = i + 1
    c[:, 640:768] = 128 - i
    c[:, 768] = 127 - j[:, 0]
    c[:, 769] = j[:, 0]
    return c


def host_inputs(cfg, core, inputs):
    TP, NP, TS, SEG = cfg["TP"], cfg["NP"], cfg["TS"], cfg["SEG"]
    f = lambda a: np.ascontiguousarray(np.asarray(a, dtype=np.float32))
    roll = core * SEG
    xs_full = np.asarray(inputs["x_sample"], dtype=np.float32)[0]
    pos = (np.arange(TS) + roll) % TS
    m = {
        "xp": f(np.asarray(inputs["x_prompt"])[core * NP:(core + 1) * NP]),
        "xs": f(np.roll(xs_full, -roll, axis=0)),
        "rcp": f(rowcol(np.arange(TP))),
        "rcs": f(rowcol(pos)),
        "ctab": const_table(),
    }
    df = np.where(pos < roll, roll - 1 - pos, BIG).astype(np.float32)
    db = np.where(pos >= roll + SEG, pos - (roll + SEG), BIG).astype(np.float32)
    own = (pos >= roll) & (pos < roll + SEG)
    df[own] = BIG
    db[own] = BIG
    d2 = np.stack([df, db], axis=-1).reshape(TS // 128, 128, 2).transpose(1, 0, 2)
    m["dists"] = f(d2)
    m["w_in"] = f(inputs["w_in"][0])
    m["b_gate"] = f(inputs["b_gate"][0]).reshape(1, 2048)
    m["q_norm"] = f(inputs["q_norm"][0]).reshape(1, DH)
    m["k_norm"] = f(inputs["k_norm"][0]).reshape(1, DH)
    m["dec_f"] = f(inputs["ret_decay_fwd"][0]).reshape(1, 4)
    m["dec_b"] = f(inputs["ret_decay_bwd"][0]).reshape(1, 4)
    m["ret_norm"] = f(inputs["ret_norm"][0]).reshape(1, D)
    m["w_ba"] = f(inputs["w_branch_attn"][0])
    m["w_br"] = f(inputs["w_branch_ret"][0])
    m["w_out"] = f(inputs["w_out"][0])
    m["norm_mix"] = f(inputs["norm_mix"][0]).reshape(1, D)
    m["norm_ffn"] = f(inputs["norm_ffn"][0]).reshape(1, D)
    m["norm_fin"] = f(inputs["norm_final"]).reshape(1, D)
    m["w_f1"] = f(inputs["w_ffn_in"][0])
    m["w_f2"] = f(inputs["w_ffn_out"][0])
    return m


_CACHE = {}


def run(cfg, inputs, ncores=8):
    key = tuple(sorted(cfg.items()))
    if key not in _CACHE:
        _CACHE[key] = build(cfg)[0]
    nc = _CACHE[key]
    in_maps = [host_inputs(cfg, c, inputs) for c in range(ncores)]
    res = run_bass_kernel_spmd(nc, in_maps, core_ids=list(range(ncores)))
    yp = np.concatenate([res.results[c]["yp"] for c in range(ncores)], axis=0)
    ysm = np.concatenate([res.results[c]["ys"] for c in range(ncores)], axis=0)[None]
    return yp.astype(np.float32), ysm.astype(np.float32)


def kernel(**inputs):
    cfg = {"TP": 2048, "NP": 2, "TS": 16384, "SEG": 2048}
    return run(cfg, inputs)
```

```python
import math
import numpy as np
from contextlib import ExitStack
import concourse.bass as bass
import concourse.mybir as mybir
from concourse.bass_utils import run_bass_kernel_spmd

F32 = mybir.dt.float32
BF16 = mybir.dt.bfloat16
AF = mybir.ActivationFunctionType
ALU = mybir.AluOpType
AX = mybir.AxisListType

D = 1024
GW = 64
NH_A, NKV, DH = 8, 2, 64
NH_R, DK, DV = 4, 128, 256
DFF = 2816
NFC = DFF // 128
EPS = 1e-6
THETA = 10000.0
G = 512
NT = G // 128
BIG = 1.0e9

WIN_BLOCKS = [("qa", 0, 512), ("kva", 512, 256), ("qr", 768, 512), ("kr", 1280, 512),
              ("vr0", 1792, 512), ("vr1", 2304, 512), ("gr0", 2816, 512), ("gr1", 3328, 512),
              ("gt0", 3840, 512), ("gt1", 4352, 512), ("gt2", 4864, 512), ("gt3", 5376, 512)]
WIN_IDX = {n: i for i, (n, _, _) in enumerate(WIN_BLOCKS)}


class T:
    def __init__(self, ap, name=""):
        self.ap = ap
        self.name = name
        self.last_w = None
        self.readers = []
        self.dead = False
        self.excl = False

    def __getitem__(self, k):
        return self.ap[k]


class Op:
    __slots__ = ("eng", "fn", "deps", "raw", "id", "is_dma", "sig", "semval")

    def __init__(self, eng, fn, is_dma):
        self.eng = eng
        self.fn = fn
        self.deps = set()
        self.raw = set()
        self.is_dma = is_dma
        self.sig = False
        self.semval = 0


class Sched:
    ENGS = ("pe", "act", "dve", "pool", "sp")

    def __init__(self, nc):
        self.nc = nc
        self.ops = []

    def op(self, eng, fn, reads=(), writes=(), dma=False):
        o = Op(eng, fn, dma)
        o.id = len(self.ops)
        for t in reads:
            assert not t.dead, ("read of dead buffer", t.name)
            if t.last_w is not None:
                o.deps.add(t.last_w)
                o.raw.add(t.last_w)
            if t.excl:
                for r in t.readers:
                    if self.ops[r].eng != eng:
                        o.deps.add(r)
        for t in writes:
            assert not t.dead, ("write of dead buffer", t.name)
            if t.last_w is not None:
                o.deps.add(t.last_w)
            o.deps.update(t.readers)
        for t in reads:
            t.readers.append(o.id)
        for t in writes:
            t.last_w = o.id
            t.readers = []
        o.deps.discard(o.id)
        self.ops.append(o)
        return o

    def pe(self, fn, reads=(), writes=()):
        return self.op("pe", fn, reads, writes)

    def act(self, fn, reads=(), writes=()):
        return self.op("act", fn, reads, writes)

    def dve(self, fn, reads=(), writes=()):
        return self.op("dve", fn, reads, writes)

    def pool(self, fn, reads=(), writes=()):
        return self.op("pool", fn, reads, writes)

    def dma(self, fn, reads=(), writes=(), q="sp"):
        return self.op(q, fn, reads, writes, dma=True)

    def emit(self, stack):
        nc = self.nc
        ops = self.ops
        engh = {"pe": nc.tensor, "act": nc.scalar, "dve": nc.vector, "pool": nc.gpsimd, "sp": nc.sync}
        waits = [None] * len(ops)
        last_waited = {e: {p: -1 for p in self.ENGS} for e in self.ENGS}
        dma_waited = {e: set() for e in self.ENGS}
        for o in ops:
            need = {}
            dl = []
            for d in o.deps:
                p = ops[d]
                if p.is_dma:
                    if d not in dma_waited[o.eng]:
                        dl.append(d)
                        dma_waited[o.eng].add(d)
                    continue
                if p.eng == o.eng:
                    if o.eng == "pe":
                        continue
                if d > need.get(p.eng, -1):
                    need[p.eng] = d
            w = []
            for pe_, d in need.items():
                if d > last_waited[o.eng][pe_]:
                    last_waited[o.eng][pe_] = d
                    w.append(d)
                    ops[d].sig = True
            for d in dl:
                w.append(d)
            waits[o.id] = w
        esem = {e: stack.enter_context(nc.semaphore("s_" + e)) for e in ("pe", "act", "dve", "pool")}
        ecnt = {e: 0 for e in esem}
        NQ = {"sp": 24, "pool": 8, "act": 8}
        dsem = {q: [stack.enter_context(nc.semaphore("d%s%d" % (q, i))) for i in range(n)] for q, n in NQ.items()}
        dcnt = {q: [0] * n for q, n in NQ.items()}
        dk = {q: 0 for q in NQ}
        dslot = {}
        for o in ops:
            if o.is_dma:
                q = o.eng
                s_ = dk[q] % NQ[q]
                dk[q] += 1
                dcnt[q][s_] += 16
                dslot[o.id] = (q, s_, dcnt[q][s_])
            elif o.sig:
                ecnt[o.eng] += 1
                o.semval = ecnt[o.eng]
        self.n_waits = 0
        for o in ops:
            h = engh[o.eng]
            for d in waits[o.id]:
                p = ops[d]
                if p.is_dma:
                    q, s_, v = dslot[d]
                    h.wait_ge(dsem[q][s_], v)
                else:
                    h.wait_ge(esem[p.eng], p.semval)
                self.n_waits += 1
            if o.is_dma:
                q, s_, v = dslot[o.id]
                if v > 16:
                    h.wait_ge(dsem[q][s_], v - 16)
                ins = o.fn(h)
                ins.then_inc(dsem[q][s_], 16)
            else:
                ins = o.fn(h)
                if o.sig:
                    ins.then_inc(esem[o.eng], 1)
        for q in NQ:
            for s_ in range(NQ[q]):
                if dcnt[q][s_] > 0:
                    nc.sync.wait_ge(dsem[q][s_], dcnt[q][s_])
        self.stats = dict(ecnt)


class Arena:
    def __init__(self, tensor, nbytes):
        self.t = tensor
        self.n = nbytes
        self.live = []

    def carve(self, lo, nbytes, name):
        hi = lo + nbytes
        assert hi <= self.n and lo % 4 == 0 and nbytes % 4 == 0, (name, lo, nbytes, self.n)
        t = T(self.t[:, lo // 2:hi // 2], name)
        inh = set()
        keep = []
        for (a, b, o) in self.live:
            if a < hi and lo < b:
                o.dead = True
                if o.last_w is not None:
                    inh.add(o.last_w)
                inh.update(o.readers)
                if a < lo:
                    keep.append((a, lo, o))
                if hi < b:
                    keep.append((hi, b, o))
            else:
                keep.append((a, b, o))
        keep.append((lo, hi, t))
        self.live = keep
        t.readers = sorted(inh)
        return t


class Psum:
    def __init__(self, tensor):
        self.t = tensor
        self.cur = [None] * 8
        self.rr = 0

    def bank(self, i, name="ps"):
        t = T(self.t[:, i, :], name)
        t.excl = True
        o = self.cur[i]
        if o is not None:
            o.dead = True
            inh = set(o.readers)
            if o.last_w is not None:
                inh.add(o.last_w)
            t.readers = sorted(inh)
        self.cur[i] = t
        t.idx = i
        return t

    def next(self, name="ps", pool=(0, 1, 2, 3, 4, 5, 6, 7)):
        i = pool[self.rr % len(pool)]
        self.rr += 1
        return self.bank(i, name)


def build(cfg):
    TP, NP, TS, SEG = cfg["TP"], cfg["NP"], cfg["TS"], cfg["SEG"]
    assert TP % G == 0 and SEG % G == 0 and TS % G == 0
    NKMAX = max(TP, TS)
    NCHMAX = max(TP, SEG) // 128
    nc = bass.Bass("TRN2", target_bir_lowering=False)

    def din(name, shape, dt=F32):
        return nc.dram_tensor(name, list(shape), dt, kind="ExternalInput").ap()

    def dscr(name, shape, dt=BF16):
        return nc.dram_tensor(name, list(shape), dt, kind="Internal").ap()

    xp = din("xp", [NP, TP, D])
    xs = din("xs", [TS, D])
    rcp = din("rcp", [128, TP // 128, 2])
    rcs = din("rcs", [128, TS // 128, 2])
    dists = din("dists", [128, TS // 128, 2])
    ctab = din("ctab", [128, 802])
    w_in = din("w_in", [D, 5888])
    b_gate = din("b_gate", [1, 2048])
    q_norm = din("q_norm", [1, DH])
    k_norm = din("k_norm", [1, DH])
    dec_f = din("dec_f", [1, 4])
    dec_b = din("dec_b", [1, 4])
    ret_norm = din("ret_norm", [1, D])
    w_ba = din("w_ba", [512, D])
    w_br = din("w_br", [D, D])
    w_out = din("w_out", [D, D])
    norm_mix = din("norm_mix", [1, D])
    norm_ffn = din("norm_ffn", [1, D])
    norm_fin = din("norm_fin", [1, D])
    w_f1 = din("w_f1", [D, 2 * DFF])
    w_f2 = din("w_f2", [DFF, D])
    yp = nc.dram_tensor("yp", [NP, TP, D], F32, kind="ExternalOutput").ap()
    ys = nc.dram_tensor("ys", [SEG, D], F32, kind="ExternalOutput").ap()

    wA = dscr("wA", [12, 128, 8 * 512])
    w1s = dscr("w1s", [11, 128, 2 * 8 * 256])
    w2s = dscr("w2s", [2, 128, NFC * 512])
    wbrs = dscr("wbrs", [2, 128, 8 * 512])
    wouts = dscr("wouts", [2, 128, 8 * 512])
    wbas = dscr("wbas", [2, 64, 8 * 512])
    tabp = dscr("tabp_s", [TP // 128, 128, 192], F32)
    tabs = dscr("tabs_s", [TS // 128, 128, 192], F32)
    kts = dscr("kts", [128, NKMAX])
    vs = dscr("vs", [NKV, 128, (NKMAX // 128) * 66])
    bns = dscr("bns", [NCHMAX, 128, 1024])

    st = ExitStack()
    S = Sched(nc)

    class StopBuild(Exception):
        pass

    def ck(name):
        if cfg.get("stop") == name:
            raise StopBuild()
    ARENA_BYTES = 206 * 1024
    arena_t = st.enter_context(nc.sbuf_tensor("arena", [128, ARENA_BYTES // 2], BF16))
    AR = Arena(arena_t, ARENA_BYTES)
    psum_t = st.enter_context(nc.psum_tensor("psum", [128, 8, 512], F32))
    PS = Psum(psum_t)
    KB = 1024

    def f32v(buf_, pat=None, **kw):
        v = buf_.ap.bitcast(F32)
        return v.rearrange(pat, **kw) if pat else v

    def b16v(buf_, pat=None, **kw):
        v = buf_.ap
        return v.rearrange(pat, **kw) if pat else v

    off = [0]

    def res(nbytes, name):
        t = AR.carve(off[0], nbytes, name)
        off[0] += nbytes
        return t

    g_mix = res(4 * KB, "g_mix")
    g_ffn = res(4 * KB, "g_ffn")
    g_fin = res(4 * KB, "g_fin")
    g_ret = res(4 * KB, "g_ret")
    maskT = res(2 * KB, "maskT")
    QDf = res(2 * KB, "QDf")
    QDb = res(2 * KB, "QDb")
    ctb = res(3208, "ctab")
    ident = res(256, "ident")
    ones_b = res(256, "ones_b")
    ones_f = res(256, "ones_f")
    gq = res(256, "gq")
    gk = res(256, "gk")
    small = res(1024, "small")
    bgb = res(4 * KB, "bgb")
    Ff = res(4 * KB, "Ff")
    Bf = res(4 * KB, "Bf")
    Fbf = [res(2 * KB, "Fbf0"), res(2 * KB, "Fbf1")]
    Bbf = [res(2 * KB, "Bbf0"), res(2 * KB, "Bbf1")]
    stats = res(1024, "stats")
    statsF = [res(256, "statsF0"), res(256, "statsF1")]
    RES_END = off[0]
    GB = (RES_END + 1023) // 1024 * 1024

    smallv = f32v(small)
    C_LGF, C_LGB, C_CDF, C_CDB, C_KDF, C_KDB, C_NB, C_NH, C_TMP = 0, 4, 8, 12, 16, 20, 24, 32, 64

    O_XT = 0
    O_TAB = 16 * KB
    O_DST = 19 * KB
    O_WS = [20 * KB, 28 * KB]
    O_HT = 36 * KB
    O_HB = 44 * KB
    O_P = 52 * KB
    O_QT = O_P
    O_QRT = O_P + 8 * KB
    O_QFT = O_P + 12 * KB
    O_QBT = O_P + 16 * KB
    O_KRT = O_P + 20 * KB
    O_KF = O_P + 24 * KB
    O_KB2 = O_P + 12 * KB
    O_VR = O_P + 28 * KB
    O_GSIL = O_P + 36 * KB
    O_BN = O_P + 44 * KB
    O_ATT = O_P + 52 * KB
    O_RETT = O_P + 60 * KB
    O_E = O_P + 68 * KB
    O_CT = O_E + 6 * KB
    O_TG = O_P
    O_M1 = O_P + 16 * KB
    O_MIX = O_P + 44 * KB
    O_MIXT = O_P + 28 * KB
    O_HID = O_P
    O_FT = O_P + 22 * KB
    O_W2 = [O_P + 30 * KB, O_P + 52 * KB]
    O_OUT = [O_P + 74 * KB, O_P + 78 * KB]
    O_STG_F = [O_P + 52 * KB]
    O_XT2 = O_P + 36 * KB
    O_TAB2 = O_P + 100 * KB
    O_STG_B = [O_P, O_P]
    O_HT2 = O_P + 92 * KB
    assert GB + O_P + 103 * KB <= ARENA_BYTES, (GB, O_P)

    def gcarve(o, n, name):
        return AR.carve(GB + o, n, name)

    def mm(out, lhsT, rhs, start, stop, reads, writes):
        S.pe(lambda e, o=out, l=lhsT, r=rhs, a=start, b=stop: e.matmul(o, lhsT=l, rhs=r, start=a, stop=b),
             reads, writes)

    def tr(out, in_, reads, writes):
        S.pe(lambda e, o=out, i=in_: e.transpose(out=o, in_=i, identity=ident[:]), list(reads) + [ident], writes)

    def act(out, in_, func, reads, writes, scale=1.0, bias=None, accum=None):
        def f(e, o=out, i=in_, fu=func, sc=scale, bi=bias, ac=accum):
            kw = {}
            if bi is not None:
                kw["bias"] = bi
            if ac is not None:
                kw["accum_out"] = ac
            return e.activation(out=o, in_=i, func=fu, scale=sc, **kw)
        S.act(f, reads, writes)

    def tt(eng, out, in0, in1, op, reads, writes):
        S.op(eng, lambda e, o=out, a=in0, b=in1, p=op: e.tensor_tensor(out=o, in0=a, in1=b, op=p), reads, writes)

    def ts(eng, out, in0, s1, op0, reads, writes, s2=None, op1=None):
        if op1 is None:
            S.op(eng, lambda e, o=out, a=in0, x=s1, p=op0: e.tensor_scalar(out=o, in0=a, scalar1=x, scalar2=None, op0=p),
                 reads, writes)
        else:
            S.op(eng, lambda e, o=out, a=in0, x=s1, y=s2, p=op0, q=op1:
                 e.tensor_scalar(out=o, in0=a, scalar1=x, scalar2=y, op0=p, op1=q), reads, writes)

    def stt(out, in0, scalar, in1, op0, op1, reads, writes):
        S.dve(lambda e, o=out, a=in0, s_=scalar, b=in1, p=op0, q=op1:
              e.scalar_tensor_tensor(out=o, in0=a, scalar=s_, in1=b, op0=p, op1=q), reads, writes)

    def cp(eng, out, in_, reads, writes):
        if eng == "act":
            act(out, in_, AF.Copy, reads, writes)
        else:
            S.op(eng, lambda e, o=out, i=in_: e.tensor_copy(out=o, in_=i), reads, writes)

    def dma(out, in_, reads, writes, q="sp"):
        S.dma(lambda e, o=out, i=in_: e.dma_start(out=o, in_=i), reads, writes, q=q)

    def memset(eng, ap, val, writes):
        S.op(eng, lambda e, a=ap, v=val: e.memset(a, v), (), writes)

    def setup():
        for (gt, src) in ((g_mix, norm_mix), (g_ffn, norm_ffn), (g_fin, norm_fin), (g_ret, ret_norm)):
            dma(f32v(gt), src.partition_broadcast(128), [], [gt])
        ts("pool", f32v(g_ret), f32v(g_ret), 0.5, ALU.mult, [g_ret], [g_ret])
        dma(f32v(ctb), ctab[:, :], [], [ctb])
        dma(f32v(gq), q_norm.partition_broadcast(128), [], [gq])
        dma(f32v(gk), k_norm.partition_broadcast(128), [], [gk])
        dma(smallv[:, C_LGF:C_LGF + 4], dec_f.partition_broadcast(128), [], [small])
        dma(smallv[:, C_LGB:C_LGB + 4], dec_b.partition_broadcast(128), [], [small])
        ck("s1")
        tmpf = gcarve(O_CT, 2 * KB, "tmp_ident")
        tf = f32v(tmpf)[:, 0:128]
        memset("pool", tf, 1.0, [tmpf])
        S.pool(lambda e: e.affine_select(out=tf, in_=tf, pattern=[[-1, 128]], compare_op=ALU.is_equal, fill=0.0,
                                         base=0, channel_multiplier=1), [tmpf], [tmpf])
        cp("dve", ident[:], tf, [tmpf], [ident])
        memset("pool", ones_b[:], 0.0, [ones_b])
        memset("pool", ones_b[0:1, :], 1.0, [ones_b])
        memset("pool", f32v(ones_f), 0.0, [ones_f])
        memset("pool", f32v(ones_f)[64:65, :], 1.0, [ones_f])
        memset("pool", bgb[:], 0.0, [bgb])
        memset("pool", smallv[:, C_NH:C_NH + 32], -0.5, [small])
        ck("s2")
        tb = gcarve(O_CT + 2 * KB, 8 * KB, "tmp_bg")
        dma(f32v(tb)[0:1, :], b_gate[:, :], [], [tb])
        cp("dve", bgb[0:1, :], f32v(tb)[0:1, :], [tb], [bgb])
        ck("s3")
        lg = smallv[:, C_LGF:C_LGF + 8]
        act(lg, lg, AF.Exp, [small], [small], scale=-1.0)
        ts("dve", lg, lg, 1.0, ALU.add, [small], [small])
        act(lg, lg, AF.Ln, [small], [small])
        ts("dve", lg, lg, -1.0, ALU.mult, [small], [small])
        ck("s4")
        cv = f32v(ctb)
        P1, P2, MGE, MLT = cv[:, 0:128], cv[:, 128:256], cv[:, 256:384], cv[:, 384:512]
        IR1, IR2, JC1, JC2 = cv[:, 512:640], cv[:, 640:768], cv[:, 768:769], cv[:, 769:770]
        act(smallv[:, C_CDF:C_CDF + 8], lg, AF.Exp, [small], [small], scale=128.0)
        act(smallv[:, C_KDF:C_KDF + 4], smallv[:, C_LGF:C_LGF + 4], AF.Exp, [small, ctb], [small], scale=JC1)
        act(smallv[:, C_KDB:C_KDB + 4], smallv[:, C_LGB:C_LGB + 4], AF.Exp, [small, ctb], [small], scale=JC2)
        ck("s5")
        mv = f32v(maskT, "p (h i) -> p h i", h=4)
        qf = f32v(QDf, "p (h i) -> p h i", h=4)
        qb = f32v(QDb, "p (h i) -> p h i", h=4)
        t1 = gcarve(O_CT + 10 * KB, 2 * KB, "tmp_m1")
        t1v = f32v(t1)[:, 0:128]
        for h in range(4):
            lf = smallv[:, C_LGF + h:C_LGF + h + 1]
            lb = smallv[:, C_LGB + h:C_LGB + h + 1]
            act(mv[:, h, :], P1, AF.Exp, [ctb, small], [maskT], scale=lf)
            tt("dve", mv[:, h, :], mv[:, h, :], MGE, ALU.mult, [maskT, ctb], [maskT])
            act(t1v, P2, AF.Exp, [ctb, small], [t1], scale=lb)
            tt("dve", t1v, t1v, MLT, ALU.mult, [t1, ctb], [t1])
            tt("dve", mv[:, h, :], mv[:, h, :], t1v, ALU.add, [maskT, t1], [maskT])
            act(qf[:, h, :], IR1, AF.Exp, [ctb, small], [QDf], scale=lf)
            act(qb[:, h, :], IR2, AF.Exp, [ctb, small], [QDb], scale=lb)
            ts("dve", qf[:, h, :], qf[:, h, :], float(DK) ** -0.5, ALU.mult, [QDf], [QDf])
            ts("dve", qb[:, h, :], qb[:, h, :], float(DK) ** -0.5, ALU.mult, [QDb], [QDb])
        ck("s6")
        mq = smallv[:, C_TMP:C_TMP + 1]
        mk = smallv[:, C_TMP + 1:C_TMP + 2]
        S.dve(lambda e: e.tensor_reduce(out=mq, in_=f32v(gq), axis=AX.X, op=ALU.max, apply_absolute_value=True),
              [gq], [small])
        S.dve(lambda e: e.tensor_reduce(out=mk, in_=f32v(gk), axis=AX.X, op=ALU.max, apply_absolute_value=True),
              [gk], [small])
        tt("dve", mq, mq, mk, ALU.mult, [small], [small])
        ts("dve", smallv[:, C_NB:C_NB + 1], mq, -8.0, ALU.mult, [small], [small])

    wT = {}
    prep_tasks = []
    stg_i = [0]
    cast_engs = ["act", "dve"]

    def prep_piece(key, src_ap, dst_ap, npart, nelem):
        st_ = {}

        def load():
            i = stg_i[0]
            stg_i[0] += 1
            sf = gcarve(O_STG_F[i % len(O_STG_F)], 16 * KB, "stg_f")
            a, b = src_ap.shape[1], src_ap.shape[2]
            fv = f32v(sf)[0:npart, 0:nelem]
            dma(fv.rearrange("p (a b) -> p a b", a=a), src_ap, [], [sf])
            st_["i"], st_["sf"], st_["fv"] = i, sf, fv

        def finish():
            i, sf, fv = st_["i"], st_["sf"], st_["fv"]
            sb_ = gcarve(O_STG_B[i % 2], 8 * KB, "stg_b")
            cp(cast_engs[i % 2], sb_[0:npart, 0:nelem], fv, [sf], [sb_])
            t = T(dst_ap, key)
            dma(dst_ap, sb_[0:npart, 0:nelem], [sb_], [t], q=("act" if cast_engs[i % 2] == "act" else "pool"))
            wT.setdefault(key, []).append(t)
        prep_tasks.append((load, finish))

    def prep_qa():
        st_ = {}

        def load():
            i = stg_i[0]
            stg_i[0] += 1
            sf = gcarve(O_STG_F[i % len(O_STG_F)], 16 * KB, "stg_f")
            fv = f32v(sf)[:, 0:4096]
            f5 = fv.rearrange("p (c h g d) -> p c h g d", c=8, h=4, g=2)
            for g in range(2):
                for h in range(4):
                    c0 = g * 256 + h * 64
                    dma(f5[:, :, h, g, :], w_in[:, c0:c0 + 64].rearrange("(c p) d -> p c d", p=128), [], [sf])
            st_["i"], st_["sf"], st_["fv"] = i, sf, fv

        def finish():
            i, sf, fv = st_["i"], st_["sf"], st_["fv"]
            sb_ = gcarve(O_STG_B[i % 2], 8 * KB, "stg_b")
            cp(cast_engs[i % 2], sb_[:, 0:4096], fv, [sf], [sb_])
            dst = wA[WIN_IDX["qa"], :, :]
            t = T(dst, "qa")
            dma(dst, sb_[:, 0:4096], [sb_], [t], q=("act" if cast_engs[i % 2] == "act" else "pool"))
            wT.setdefault("qa", []).append(t)
        prep_tasks.append((load, finish))

    def make_prep(order):
        for name in order:
            if name == "qa":
                prep_qa()
            elif name in WIN_IDX:
                bi = WIN_IDX[name]
                _, c0, w = WIN_BLOCKS[bi]
                prep_piece(name, w_in[:, c0:c0 + w].rearrange("(c p) n -> p c n", p=128), wA[bi, :, 0:8 * w], 128, 8 * w)
            elif name.startswith("w1_"):
                fb = int(name[3:])
                for gu in range(2):
                    c0 = gu * DFF + fb * 256
                    prep_piece(name, w_f1[:, c0:c0 + 256].rearrange("(c p) n -> p c n", p=128),
                               w1s[fb, :, gu * 2048:(gu + 1) * 2048], 128, 2048)
            elif name.startswith("w2_"):
                hf = int(name[3:])
                for (f0, f1) in ((0, 6), (6, 12), (12, 17), (17, 22)):
                    prep_piece(name, w_f2[f0 * 128:f1 * 128, hf * 512:(hf + 1) * 512].rearrange("(f p) n -> p f n", p=128),
                               w2s[hf, :, f0 * 512:f1 * 512], 128, (f1 - f0) * 512)
            elif name.startswith("wbr_") or name.startswith("wout_"):
                hf = int(name.split("_")[1])
                src, dst = (w_br, wbrs) if name.startswith("wbr_") else (w_out, wouts)
                prep_piece(name, src[:, hf * 512:(hf + 1) * 512].rearrange("(c p) n -> p c n", p=128),
                           dst[hf, :, :], 128, 4096)
            elif name.startswith("wba_"):
                hf = int(name[4:])
                prep_piece(name, w_ba[:, hf * 512:(hf + 1) * 512].rearrange("(h p) n -> p h n", p=64),
                           wbas[hf, :, :], 64, 4096)

    prep_pending = [None]

    def run_prep(n):
        for _ in range(n):
            if prep_pending[0] is not None:
                prep_pending[0]()
                prep_pending[0] = None
            elif prep_tasks:
                ld, fin = prep_tasks.pop(0)
                ld()
                fin()

    def prep_step():
        if prep_pending[0] is not None:
            prep_pending[0]()
            prep_pending[0] = None
        if prep_tasks:
            ld, fin = prep_tasks.pop(0)
            ld()
            prep_pending[0] = fin

    ws_i = [0]

    def load_w(key, src_ap, npart, nelem):
        i = ws_i[0]
        ws_i[0] += 1
        t = gcarve(O_WS[i % 2], 8 * KB, "ws")
        v = t.ap[0:npart, 0:nelem]
        dma(v, src_ap, wT[key], [t])
        return t, v

    ev_i = [0]

    def evac_eng():
        ev_i[0] += 1
        return "act" if ev_i[0] % 2 == 0 else "dve"

    def rstd_from_ss(ss_ap, out_ap, n, width, reads_t, writes_t):
        ts("pool", out_ap, ss_ap, 1.0 / width, ALU.mult, reads_t, writes_t, s2=EPS, op1=ALU.add)
        tt("pool", out_ap, out_ap, smallv[:, C_NH:C_NH + n], ALU.pow, list(writes_t) + [small], writes_t)

    def norm_transpose(xt, gain_t, hT, sscol):
        xv = f32v(xt, "p (t d) -> p t d", t=NT)
        sv = f32v(stats)
        hb = gcarve(O_HB, 8 * KB, "hb")
        hbv = b16v(hb, "p (t d) -> p t d", t=NT)
        hTv = b16v(hT, "p (c n) -> p c n", c=8)
        for t in range(NT):
            act(hbv[:, t, :], xv[:, t, :], AF.Square, [xt], [hb, stats], accum=sv[:, sscol + t:sscol + t + 1])
        rstd_from_ss(sv[:, sscol:sscol + NT], sv[:, sscol + 4:sscol + 4 + NT], NT, float(D), [stats], [stats])
        for t in range(NT):
            stt(hbv[:, t, :], xv[:, t, :], sv[:, sscol + 4 + t:sscol + 5 + t], f32v(gain_t), ALU.mult, ALU.mult,
                [xt, stats, gain_t], [hb])
            pb = PS.next("tr")
            pv = pb.ap.bitcast(BF16)
            for c in range(8):
                tr(pv[:, c * 128:(c + 1) * 128], hbv[:, t, c * 128:(c + 1) * 128], [hb], [pb])
            cp(evac_eng(), hTv[:, :, t * 128:(t + 1) * 128], pv.rearrange("p (c n) -> p c n", c=8), [pb], [hT])

    def load_x(x_ap, tile0, slot):
        xt = gcarve(O_XT if slot == 0 else O_XT2, 16 * KB, "xt")
        dma(f32v(xt, "p (t d) -> p t d", t=NT), x_ap[tile0 * 128:(tile0 + NT) * 128, :].rearrange("(t p) d -> p t d", p=128),
            [], [xt])
        return xt

    tabT = {}
    TCH = 16

    def load_tab(tab_ap, tile0, slot):
        tb = gcarve(O_TAB if slot == 0 else O_TAB2, 3 * KB, "tab")
        key = "p" if tab_ap is tabp else "s"
        dma(f32v(tb, "p (t c) -> p t c", t=NT), tab_ap[tile0:tile0 + NT, :, :].rearrange("t p c -> p t c"),
            [tabT[(key, tile0 // TCH)]], [tb])
        return tb

    def gen_tables():
        TWO_PI = 2.0 * math.pi
        cv = f32v(ctb)
        iot = cv[:, 770:802]
        invr = smallv[:, 96:128]
        inva = smallv[:, 128:144]
        act(invr, iot, AF.Exp, [ctb], [small], scale=-math.log(THETA) / 32.0)
        act(inva, iot[:, 0:16], AF.Exp, [ctb], [small], scale=-math.log(THETA) / 16.0)
        ts("dve", smallv[:, 96:144], smallv[:, 96:144], 1.0 / TWO_PI, ALU.mult, [small], [small])
        W = 96
        GCH = 2 * TCH
        for (key, rc_ap, dst, ntile) in (("p", rcp, tabp, TP // 128), ("s", rcs, tabs, TS // 128)):
            for c0 in range(0, ntile, GCH):
                n = min(GCH, ntile - c0)
                E = n * W
                rc = gcarve(O_P, 256, "rc")
                rcv = f32v(rc)[:, 0:n * 2].rearrange("p (t two) -> p t two", two=2)
                dma(rcv, rc_ap[:, c0:c0 + n, :], [], [rc])
                ang = gcarve(O_P + 1 * KB, 12 * KB, "ang")
                a3 = f32v(ang)[:, 0:E].rearrange("p (t w) -> p t w", w=W)
                for (lo, hi, inv, col) in ((0, 32, invr, 0), (32, 64, invr, 1), (64, 80, inva, 0), (80, 96, inva, 1)):
                    wdt = hi - lo
                    tt("dve", a3[:, :, lo:hi], rcv[:, :, col:col + 1].broadcast_to([128, n, wdt]),
                       inv.unsqueeze(1).broadcast_to([128, n, wdt]), ALU.mult, [rc, small], [ang])
                tbl = gcarve(O_P + 52 * KB, 24 * KB, "tbl")
                t3 = f32v(tbl)[:, 0:n * 192].rearrange("p (t c) -> p t c", c=192)
                for (shift, o_r, o_a) in ((0.25, 0, 128), (0.0, 64, 160)):
                    xs_ = gcarve(O_P + 13 * KB, 12 * KB, "xs")
                    xv_ = f32v(xs_)[:, 0:E]
                    ki = gcarve(O_P + 25 * KB, 12 * KB, "ki")
                    kiv = ki.ap.bitcast(mybir.dt.int32)[:, 0:E]
                    kf = gcarve(O_P + 37 * KB, 12 * KB, "kf")
                    kfv = f32v(kf)[:, 0:E]
                    if shift != 0.0:
                        ts("dve", xv_, f32v(ang)[:, 0:E], shift, ALU.add, [ang], [xs_])
                        src_t, src_v = xs_, xv_
                    else:
                        src_t, src_v = ang, f32v(ang)[:, 0:E]
                    cp("dve", kiv, src_v, [src_t], [ki])
                    cp("dve", kfv, kiv, [ki], [kf])
                    tt("dve", xv_, src_v, kfv, ALU.subtract, [src_t, kf], [xs_])
                    ts("dve", kfv, xv_, 0.5, ALU.is_gt, [xs_], [kf])
                    tt("dve", xv_, xv_, kfv, ALU.subtract, [xs_, kf], [xs_])
                    x3 = xv_.rearrange("p (t w) -> p t w", w=W)
                    act(t3[:, :, o_r:o_r + 64], x3[:, :, 0:64], AF.Sin, [xs_], [tbl], scale=TWO_PI)
                    act(t3[:, :, o_a:o_a + 32], x3[:, :, 64:96], AF.Sin, [xs_], [tbl], scale=TWO_PI)
                tT = T(dst, "tab_%s%d" % (key, c0 // TCH))
                dma(dst[c0:c0 + n, :, :].rearrange("t p c -> p t c"), t3, [tbl], [tT], q="act")
                for j in range(0, n, TCH):
                    tabT[(key, (c0 + j) // TCH)] = tT


    def load_group(x_ap, tab_ap, tile0, slot=0):
        xt = gcarve(O_XT if slot == 0 else O_XT2, 16 * KB, "xt")
        dma(f32v(xt, "p (t d) -> p t d", t=NT), x_ap[tile0 * 128:(tile0 + NT) * 128, :].rearrange("(t p) d -> p t d", p=128),
            [], [xt])
        tb = gcarve(O_TAB if slot == 0 else O_TAB2, 3 * KB, "tab")
        dma(f32v(tb, "p (t c) -> p t c", t=NT), tab_ap[tile0:tile0 + NT, :, :].rearrange("t p c -> p t c"), [], [tb])
        return xt, tb

    def proj_block(hT, wv, width, t, extra=None):
        hTv = b16v(hT, "p (c n) -> p c n", c=8)
        wt, wview = wv
        w3 = wview.rearrange("p (c n) -> p c n", c=8)
        pb = PS.next("proj")
        for c in range(8):
            mm(pb.ap[:, 0:width], hTv[:, c, t * 128:(t + 1) * 128], w3[:, c, :], c == 0, (c == 7 and extra is None),
               [hT, wt], [pb])
        if extra is not None:
            lhsT, rhs, rd = extra
            mm(pb.ap[:, 0:width], lhsT, rhs, False, True, rd, [pb])
        return pb

    quad_i = [0]

    def proj_quad(hT, wv, width, extra=None):
        base = 4 * (quad_i[0] % 2)
        quad_i[0] += 1
        banks = [PS.bank(base + i, "pq") for i in range(4)]
        hTv = b16v(hT, "p (c n) -> p c n", c=8)
        wt, wview = wv
        w3 = wview.rearrange("p (c n) -> p c n", c=8)
        for t in range(NT):
            for c in range(8):
                mm(banks[t].ap[:, 0:width], hTv[:, c, t * 128:(t + 1) * 128], w3[:, c, :], c == 0,
                   (c == 7 and extra is None), [hT, wt], [banks[t]])
            if extra is not None:
                lhsT, rhs, rd_ = extra
                mm(banks[t].ap[:, 0:width], lhsT, rhs, False, True, rd_, [banks[t]])
        return banks, psum_t[:, base:base + 4, :]

    def rope4(eng, src_ts, src4, dst_t, dst4, H, hd2, cos4, sin4, tab_t, t1, t2, nt=NT):
        s5 = src4.rearrange("p t (h i two) -> p t h i two", h=H, two=2)
        d5 = dst4.rearrange("p t (h i two) -> p t h i two", h=H, two=2)
        e_, o_ = s5[:, :, :, :, 0], s5[:, :, :, :, 1]
        c4 = cos4.unsqueeze(2).broadcast_to([128, nt, H, hd2])
        s4 = sin4.unsqueeze(2).broadcast_to([128, nt, H, hd2])
        n = nt * H * hd2
        a4 = f32v(t1)[:, 0:n].rearrange("p (t h i) -> p t h i", t=nt, h=H)
        b4 = f32v(t2)[:, 0:n].rearrange("p (t h i) -> p t h i", t=nt, h=H)
        rs = list(src_ts) + [tab_t]
        tt(eng, a4, e_, c4, ALU.mult, rs, [t1])
        tt(eng, b4, o_, s4, ALU.mult, rs, [t2])
        tt(eng, d5[:, :, :, :, 0], a4, b4, ALU.subtract, [t1, t2], [dst_t])
        tt(eng, a4, e_, s4, ALU.mult, rs, [t1])
        tt(eng, b4, o_, c4, ALU.mult, rs, [t2])
        tt(eng, d5[:, :, :, :, 1], a4, b4, ALU.add, [t1, t2], [dst_t])

    ct_i = [0]

    def ctmp(name="ct"):
        i = ct_i[0]
        ct_i[0] += 1
        return gcarve(O_CT + (i % 8) * 2 * KB, 2 * KB, name)

    def rope(eng, src_t, src_v, dst_t, dst_v, H, hd2, cosv, sinv, tab_t):
        s4 = src_v.rearrange("p (h i two) -> p h i two", h=H, two=2)
        d4 = dst_v.rearrange("p (h i two) -> p h i two", h=H, two=2)
        e_, o_ = s4[:, :, :, 0], s4[:, :, :, 1]
        c3 = cosv.unsqueeze(1).broadcast_to([128, H, hd2])
        s3 = sinv.unsqueeze(1).broadcast_to([128, H, hd2])
        n = H * hd2
        ta, tb_ = ctmp("rp_a"), ctmp("rp_b")
        a3 = f32v(ta)[:, 0:n].rearrange("p (h i) -> p h i", h=H)
        b3 = f32v(tb_)[:, 0:n].rearrange("p (h i) -> p h i", h=H)
        tt(eng, a3, e_, c3, ALU.mult, [src_t, tab_t], [ta])
        tt(eng, b3, o_, s3, ALU.mult, [src_t, tab_t], [tb_])
        tt(eng, d4[:, :, :, 0], a3, b3, ALU.subtract, [ta, tb_], [dst_t])
        tc_, td = ctmp("rp_c"), ctmp("rp_d")
        c3b = f32v(tc_)[:, 0:n].rearrange("p (h i) -> p h i", h=H)
        d3b = f32v(td)[:, 0:n].rearrange("p (h i) -> p h i", h=H)
        tt(eng, c3b, e_, s3, ALU.mult, [src_t, tab_t], [tc_])
        tt(eng, d3b, o_, c3, ALU.mult, [src_t, tab_t], [td])
        tt(eng, d4[:, :, :, 1], c3b, d3b, ALU.add, [tc_, td], [dst_t])

    def head_rms(src_t, src_v, H, gain_t, sscol):
        sv = f32v(stats)
        n = H * DH
        sq = ctmp("sq")
        sqv = f32v(sq)[:, 0:n]
        tt("dve", sqv, src_v, src_v, ALU.mult, [src_t], [sq])
        S.dve(lambda e, o=sv[:, sscol:sscol + H], i=sqv.rearrange("p (h d) -> p h d", h=H):
              e.tensor_reduce(out=o, in_=i, axis=AX.X, op=ALU.add), [sq], [stats])
        rstd_from_ss(sv[:, sscol:sscol + H], sv[:, sscol + 8:sscol + 8 + H], H, float(DH), [stats], [stats])
        s3 = src_v.rearrange("p (h d) -> p h d", h=H)
        tt("dve", s3, s3, sv[:, sscol + 8:sscol + 8 + H].unsqueeze(2).broadcast_to([128, H, DH]), ALU.mult,
           [src_t, stats], [src_t])
        tt("pool", s3, s3, f32v(gain_t).unsqueeze(1).broadcast_to([128, H, DH]), ALU.mult, [src_t, gain_t], [src_t])

    dramT = {}

    def dT(name, idx, ap):
        k = (name, idx)
        if k not in dramT:
            dramT[k] = T(ap, "%s%d" % (name, idx))
        return dramT[k]

    def kva_block(hT, tb, key_tile0, ktT_list, vT_list, wv):
        tbv = f32v(tb, "p (t c) -> p t c", t=NT)
        kst = gcarve(O_E, 1 * KB, "kst")
        vst = gcarve(O_E + 1 * KB, 1056 + 1056, "vst")
        kstv = b16v(kst)
        vstv = b16v(vst)[:, 0:NT * NKV * 66].rearrange("p (t k c) -> p t k c", t=NT, k=NKV)
        memset("pool", vstv[:, :, :, 64:66], 1.0, [vst])
        sv = f32v(stats)
        banks, P4 = proj_quad(hT, wv, 256)
        cp("dve", vstv[:, :, :, 0:64], P4[:, :, 128:256].rearrange("p t (k c) -> p t k c", k=NKV), banks, [vst])
        kg = gcarve(O_E + 5 * KB, 2 * KB, "kg")
        kg3 = f32v(kg)[:, 0:512].rearrange("p (t n) -> p t n", t=NT)
        kg4 = f32v(kg)[:, 0:512].rearrange("p (t k d) -> p t k d", t=NT, k=NKV)
        kc = gcarve(O_E + 7 * KB, 2 * KB, "kc")
        kc3 = f32v(kc)[:, 0:512].rearrange("p (t n) -> p t n", t=NT)
        kc4 = f32v(kc)[:, 0:512].rearrange("p (t k d) -> p t k d", t=NT, k=NKV)
        cp("act", kc3, P4[:, :, 0:128], banks, [kc])
        act(kg3, P4[:, :, 0:128], AF.Square, banks, [kg])
        S.dve(lambda e, o=sv[:, 16:24], i=f32v(kg)[:, 0:512].rearrange("p (a d) -> p a d", d=DH):
              e.tensor_reduce(out=o, in_=i, axis=AX.X, op=ALU.add), [kg], [stats])
        rstd_from_ss(sv[:, 16:24], sv[:, 24:32], 8, float(DH), [stats], [stats])
        tt("dve", kg4, kc4,
           sv[:, 24:32].rearrange("p (t k) -> p t k", t=NT).unsqueeze(3).broadcast_to([128, NT, NKV, DH]), ALU.mult,
           [kc, stats], [kg])
        tt("dve", kg4, kg4, f32v(gk).unsqueeze(1).unsqueeze(1).broadcast_to([128, NT, NKV, DH]), ALU.mult, [kg, gk], [kg])
        kab = gcarve(O_E + 3584, 1 * KB, "kab")
        kab3 = b16v(kab)[:, 0:512].rearrange("p (t n) -> p t n", t=NT)
        r1 = gcarve(O_E + 12 * KB, 2 * KB, "r1")
        r2 = gcarve(O_E + 14 * KB, 2 * KB, "r2")
        rope4("dve", [kg], kg3, kab, kab3, NKV, 32, tbv[:, :, 128:160], tbv[:, :, 160:192], tb, r1, r2)
        p2 = PS.next("trk")
        p2v = p2.ap.bitcast(BF16)
        for t in range(NT):
            tr(p2v[:, t * 128:(t + 1) * 128], kab3[:, t, :], [kab], [p2])
        cp("act", kstv[:, 0:512], p2v[:, 0:512], [p2], [kst])
        ck("k5")
        kt_t = dT("kt", key_tile0 // NT, kts)
        v_t = dT("v", key_tile0 // NT, vs)
        k0 = key_tile0 * 128
        dma(kts[:, k0:k0 + G], kstv, [kst], [kt_t], q="act")
        for k in range(NKV):
            dma(vs[k, :, key_tile0 * 66:(key_tile0 + NT) * 66].rearrange("p (t c) -> p t c", t=NT), vstv[:, :, k, :],
                [vst], [v_t], q="act")
        ktT_list.append(kt_t)
        vT_list.append(v_t)

    def kr_vr_blocks(hT, tb, want_kr_T, wv):
        tbv = f32v(tb, "p (t c) -> p t c", t=NT)
        krb = gcarve(O_KRT if not want_kr_T else O_QBT, 4 * KB, "krb")
        krbv = b16v(krb, "p (t n) -> p t n", t=NT)
        banks, P4 = proj_quad(hT, wv, 512)
        wv0 = load_w("vr0", wA[WIN_IDX["vr0"], :, :], 128, 4096)
        krc = gcarve(O_E + 16 * KB, 8 * KB, "krc")
        krc3 = f32v(krc, "p (t n) -> p t n", t=NT)
        cp("act", krc3, P4, banks, [krc])
        r1 = gcarve(O_E + 12 * KB, 2 * KB, "r1")
        r2 = gcarve(O_E + 14 * KB, 2 * KB, "r2")
        for t0_ in (0, 2):
            rope4("dve", [krc], krc3[:, t0_:t0_ + 2, :], krb, krbv[:, t0_:t0_ + 2, :], NH_R, 64,
                  tbv[:, t0_:t0_ + 2, 0:64], tbv[:, t0_:t0_ + 2, 64:128], tb, r1, r2, nt=2)
        vr = gcarve(O_VR, 8 * KB, "vr")
        vrv = b16v(vr, "p (t n) -> p t n", t=NT)
        banks, P4 = proj_quad(hT, wv0, 512)
        wv1 = load_w("vr1", wA[WIN_IDX["vr1"], :, :], 128, 4096)
        cp("act", vrv[:, :, 0:512], P4, banks, [vr])
        banks, P4 = proj_quad(hT, wv1, 512)
        cp("act", vrv[:, :, 512:1024], P4, banks, [vr])
        return krb, vr

    def others_gen(og, ktT_list, vT_list, xts):
        tile0 = (SEG // 128) + og * NT
        xt = xts[og]
        tb = load_tab(tabs, tile0, og % 2)
        dt_ = gcarve(O_DST + (og % 2) * 64, 64, "dst")
        dtv = f32v(dt_)[:, 0:NT * 2].rearrange("p (t two) -> p t two", two=2)
        dma(dtv, dists[:, tile0:tile0 + NT, :], [], [dt_])
        hT = gcarve(O_HT if og % 2 == 0 else O_HT2, 8 * KB, "hT")
        wkva = load_w("kva", wA[WIN_IDX["kva"], :, 0:8 * 256], 128, 8 * 256)
        wkr = load_w("kr", wA[WIN_IDX["kr"], :, :], 128, 4096)
        norm_transpose(xt, g_mix, hT, 0)
        yield
        kva_block(hT, tb, tile0, ktT_list, vT_list, wkva)
        krb, vr = kr_vr_blocks(hT, tb, False, wkr)
        yield
        krv = b16v(krb, "p (t h d) -> p t h d", t=NT, h=NH_R)
        vrv = b16v(vr, "p (t n) -> p t n", t=NT)
        sv = f32v(stats)
        for t in range(NT):
            act(sv[:, 32 + t * 4:36 + t * 4], smallv[:, C_LGF:C_LGF + 4], AF.Exp, [small, dt_], [stats],
                scale=dtv[:, t, 0:1])
            act(sv[:, 48 + t * 4:52 + t * 4], smallv[:, C_LGB:C_LGB + 4], AF.Exp, [small, dt_], [stats],
                scale=dtv[:, t, 1:2])
        kf = gcarve(O_KF, 4 * KB, "Kfo")
        kb_ = gcarve(O_KB2, 4 * KB, "Kbo")
        kfv = b16v(kf, "p (t h d) -> p t h d", t=NT, h=NH_R)
        kbv = b16v(kb_, "p (t h d) -> p t h d", t=NT, h=NH_R)
        tt("dve", kfv, krv, sv[:, 32:48].rearrange("p (t h) -> p t h", t=NT).unsqueeze(3).broadcast_to([128, NT, NH_R, DK]),
           ALU.mult, [krb, stats], [kf])
        tt("dve", kbv, krv, sv[:, 48:64].rearrange("p (t h) -> p t h", t=NT).unsqueeze(3).broadcast_to([128, NT, NH_R, DK]),
           ALU.mult, [krb, stats], [kb_])
        Fv = f32v(Ff, "p (h e) -> p h e", h=NH_R)
        Bv = f32v(Bf, "p (h e) -> p h e", h=NH_R)
        for (kx, kxv, acc_t, accv) in ((kf, kfv, Ff, Fv), (kb_, kbv, Bf, Bv)):
            for hp in range(2):
                pb = PS.next("kvo")
                for hh in range(2):
                    h = hp * 2 + hh
                    for t in range(NT):
                        mm(pb.ap[:, hh * 256:(hh + 1) * 256], kxv[:, t, h, :], vrv[:, t, h * DV:(h + 1) * DV],
                           t == 0, t == NT - 1, [kx, vr], [pb])
                tt("dve", accv[:, hp * 2:hp * 2 + 2, :], accv[:, hp * 2:hp * 2 + 2, :],
                   pb.ap.rearrange("p (h e) -> p h e", h=2), ALU.add, [acc_t, pb], [acc_t])

    def pipeline(gens, n, after_a, after_group):
        next(gens[0])
        after_a(0)
        for k in range(n):
            next(gens[k])
            if k + 1 < n:
                next(gens[k + 1])
                after_a(k + 1)
            next(gens[k], None)
            after_group(k)

    def pass1_gen(x_ap, tab_ap, g, k_idx, key_tile_base, ktT_list, vT_list, bnT, bpar, xts):
        tile0 = g * NT
        xt = xts[k_idx]
        tb = load_tab(tab_ap, tile0, k_idx % 2)
        hT = gcarve(O_HT if k_idx % 2 == 0 else O_HT2, 8 * KB, "hT")
        wkva = load_w("kva", wA[WIN_IDX["kva"], :, 0:8 * 256], 128, 8 * 256)
        wkr = load_w("kr", wA[WIN_IDX["kr"], :, :], 128, 4096)
        ck("p1x")
        norm_transpose(xt, g_mix, hT, 0)
        yield
        kva_block(hT, tb, key_tile_base + tile0, ktT_list, vT_list, wkva)
        krb, vr = kr_vr_blocks(hT, tb, False, wkr)
        yield
        krv = b16v(krb, "p (t h d) -> p t h d", t=NT, h=NH_R)
        vrv = b16v(vr, "p (t n) -> p t n", t=NT)
        kb_ = gcarve(O_KB2, 4 * KB, "Kb")
        kbv = b16v(kb_, "p (t h d) -> p t h d", t=NT, h=NH_R)
        tt("dve", kbv, krv, smallv[:, C_KDB:C_KDB + 4].unsqueeze(1).unsqueeze(3).broadcast_to([128, NT, NH_R, DK]),
           ALU.mult, [krb, small], [kb_])
        Bv = f32v(Bf, "p (h e) -> p h e", h=NH_R)
        for t in reversed(range(NT)):
            n = tile0 + t
            cur = Bbf[bpar[0] % 2]
            bt = dT("bn", n, bns)
            dma(bns[n, :, :], cur[:, :], [cur], [bt], q="act")
            bnT[n] = bt
            for hp in range(2):
                pb = PS.next("kvb")
                for hh in range(2):
                    h = hp * 2 + hh
                    mm(pb.ap[:, hh * 256:(hh + 1) * 256], kbv[:, t, h, :], vrv[:, t, h * DV:(h + 1) * DV], True, True,
                       [kb_, vr], [pb])
                for hh in range(2):
                    h = hp * 2 + hh
                    stt(Bv[:, h, :], Bv[:, h, :], smallv[:, C_CDB + h:C_CDB + h + 1], pb.ap[:, hh * 256:(hh + 1) * 256],
                        ALU.mult, ALU.add, [Bf, small, pb], [Bf])
            bpar[0] += 1
            nxt = Bbf[bpar[0] % 2]
            cp("act", nxt[:, :], f32v(Bf), [Bf], [nxt])

    def pass2_group(x_ap, tab_ap, out_ap, g, nkeys, ktT_list, vT_list, bnT, fpar):
        tile0 = g * NT
        xt, tb = load_group(x_ap, tab_ap, tile0)
        tbv = f32v(tb, "p (t c) -> p t c", t=NT)
        xv = f32v(xt, "p (t d) -> p t d", t=NT)
        bng = gcarve(O_BN, 8 * KB, "bng")
        bngv = b16v(bng, "p (t n) -> p t n", t=NT)
        dma(bngv, bns[tile0:tile0 + NT, :, :].rearrange("t p n -> p t n"), [bnT[tile0 + t] for t in range(NT)], [bng])
        hT = gcarve(O_HT, 8 * KB, "hT")
        norm_transpose(xt, g_mix, hT, 0)
        sv = f32v(stats)

        QT = gcarve(O_QT, 8 * KB, "QT")
        QTv = b16v(QT, "p (h n) -> p h n", h=NH_A)
        memset("pool", QTv[0:64, 4:8, :], 0.0, [QT])
        memset("pool", QTv[64:128, 0:4, :], 0.0, [QT])
        tb4 = tbv
        r1 = gcarve(O_E + 12 * KB, 4 * KB, "r1")
        r2 = gcarve(O_E + 16 * KB, 4 * KB, "r2")
        big = gcarve(O_E + 4 * KB, 8 * KB, "cbig")
        big3 = f32v(big, "p (t n) -> p t n", t=NT)
        big4 = f32v(big, "p (t h d) -> p t h d", t=NT, h=NH_A)
        rb = gcarve(O_E + 20 * KB, 4 * KB, "rb")
        rb3 = b16v(rb, "p (t n) -> p t n", t=NT)

        def tr_pairs(src3, reads, bank_ids):
            outs = []
            for tp_ in range(2):
                p2 = PS.bank(bank_ids[tp_], "trc")
                p2v = p2.ap.bitcast(BF16)
                for tl in range(2):
                    t = tp_ * 2 + tl
                    for j in range(4):
                        tr(p2v[:, (tl * 4 + j) * 128:(tl * 4 + j + 1) * 128], src3[:, t, j * 128:(j + 1) * 128], reads, [p2])
                outs.append((p2, p2v.rearrange("p (t h n) -> p h t n", t=2, h=4)))
            return outs

        wv = load_w("qa", wA[WIN_IDX["qa"], :, :], 128, 4096)
        banks_qa, P4_qa = proj_quad(hT, wv, 512)
        act(big3, P4_qa, AF.Square, banks_qa, [big])
        S.dve(lambda e, o=sv[:, 128:160], i=f32v(big).rearrange("p (a d) -> p a d", d=DH):
              e.tensor_reduce(out=o, in_=i, axis=AX.X, op=ALU.add), [big], [stats])
        rstd_from_ss(sv[:, 128:160], sv[:, 160:192], 32, float(DH), [stats], [stats])
        qrT = gcarve(O_QRT, 4 * KB, "qrT")
        QfT = gcarve(O_QFT, 4 * KB, "QfT")
        QbT = gcarve(O_QBT, 4 * KB, "QbT")
        qrTv = b16v(qrT, "p (h n) -> p h n", h=NH_R)
        QfTv = b16v(QfT, "p (h n) -> p h n", h=NH_R)
        QbTv = b16v(QbT, "p (h n) -> p h n", h=NH_R)
        wv = load_w("qr", wA[WIN_IDX["qr"], :, :], 128, 4096)
        rq = gcarve(O_E, 4 * KB, "rq")
        rq3 = b16v(rq, "p (t n) -> p t n", t=NT)
        banks, P4 = proj_quad(hT, wv, 512)
        rope4("dve", banks, P4, rq, rq3, NH_R, 64, tb4[:, :, 0:64], tb4[:, :, 64:128], tb, r1, r2)
        qdf = f32v(QDf, "p (h i) -> p h i", h=4).unsqueeze(2).broadcast_to([128, 4, 2, 128])
        qdb = f32v(QDb, "p (h i) -> p h i", h=4).unsqueeze(2).broadcast_to([128, 4, 2, 128])
        for tp_, (p2, pv4) in enumerate(tr_pairs(rq3, [rq], (banks[0].idx, banks[1].idx))):
            tsl = slice(tp_ * 256, (tp_ + 1) * 256)
            act(qrTv[:, :, tsl].rearrange("p h (t n) -> p h t n", t=2), pv4, AF.Copy, [p2], [qrT], scale=float(DK) ** -0.5)
            tt("dve", QfTv[:, :, tsl].rearrange("p h (t n) -> p h t n", t=2), pv4, qdf, ALU.mult, [p2, QDf], [QfT])
            tt("dve", QbTv[:, :, tsl].rearrange("p h (t n) -> p h t n", t=2), pv4, qdb, ALU.mult, [p2, QDb], [QbT])
        tt("dve", big4, P4_qa.rearrange("p t (h d) -> p t h d", h=NH_A),
           sv[:, 160:192].rearrange("p (t h) -> p t h", t=NT).unsqueeze(3).broadcast_to([128, NT, NH_A, DH]), ALU.mult,
           list(banks_qa) + [stats], [big])
        tt("dve", big4, big4, f32v(gq).unsqueeze(1).unsqueeze(1).broadcast_to([128, NT, NH_A, DH]), ALU.mult, [big, gq], [big])
        rope4("dve", [big], big3, rb, rb3, NH_A, 32, tb4[:, :, 128:160], tb4[:, :, 160:192], tb, r1, r2)
        for tp_, (p2, pv4) in enumerate(tr_pairs(rb3, [rb], (banks_qa[0].idx, banks_qa[1].idx))):
            tsl = slice(tp_ * 256, (tp_ + 1) * 256)
            cp("act", QTv[0:64, 0:4, tsl].rearrange("p h (t n) -> p h t n", t=2), pv4[0:64], [p2], [QT])
            cp("dve", QTv[64:128, 4:8, tsl].rearrange("p h (t n) -> p h t n", t=2), pv4[64:128], [p2], [QT])
        krb = gcarve(O_E, 4 * KB, "krb2")
        krbv = b16v(krb, "p (t n) -> p t n", t=NT)
        krT = gcarve(O_KRT, 4 * KB, "krT")
        krTv = b16v(krT, "p (h n) -> p h n", h=NH_R)
        Kf = gcarve(O_KF, 4 * KB, "Kf")
        Kfv = b16v(Kf, "p (t h d) -> p t h d", t=NT, h=NH_R)
        wv = load_w("kr", wA[WIN_IDX["kr"], :, :], 128, 4096)
        banks, P4 = proj_quad(hT, wv, 512)
        rope4("dve", banks, P4, krb, krbv, NH_R, 64, tb4[:, :, 0:64], tb4[:, :, 64:128], tb, r1, r2)
        for tp_, (p2, pv4) in enumerate(tr_pairs(krbv, [krb], (banks[0].idx, banks[1].idx))):
            tsl = slice(tp_ * 256, (tp_ + 1) * 256)
            cp("act", krTv[:, :, tsl].rearrange("p h (t n) -> p h t n", t=2), pv4, [p2], [krT])
        tt("pool", Kfv, b16v(krb, "p (t h d) -> p t h d", t=NT, h=NH_R),
           smallv[:, C_KDF:C_KDF + 4].unsqueeze(1).unsqueeze(3).broadcast_to([128, NT, NH_R, DK]), ALU.mult, [krb, small], [Kf])
        vr = gcarve(O_VR, 8 * KB, "vr")
        vrv = b16v(vr, "p (t n) -> p t n", t=NT)
        for half in range(2):
            wv = load_w("vr%d" % half, wA[WIN_IDX["vr%d" % half], :, :], 128, 4096)
            banks, P4 = proj_quad(hT, wv, 512)
            cp("act", vrv[:, :, half * 512:(half + 1) * 512], P4, banks, [vr])
        Fv = f32v(Ff, "p (h e) -> p h e", h=NH_R)
        Fb4 = gcarve(O_HB, 8 * KB, "Fb4")
        Fb4v = b16v(Fb4, "p (t h e) -> p t h e", t=NT, h=NH_R)
        for t in range(NT):
            cp("act", Fb4v[:, t], Fv, [Ff], [Fb4])
            for hp in range(2):
                pk = PS.next("kvf")
                for hh in range(2):
                    h = hp * 2 + hh
                    mm(pk.ap[:, hh * 256:(hh + 1) * 256], Kfv[:, t, h, :], vrv[:, t, h * DV:(h + 1) * DV], True, True,
                       [Kf, vr], [pk])
                for hh in range(2):
                    h = hp * 2 + hh
                    stt(Fv[:, h, :], Fv[:, h, :], smallv[:, C_CDF + h:C_CDF + h + 1], pk.ap[:, hh * 256:(hh + 1) * 256],
                        ALU.mult, ALU.add, [Ff, small, pk], [Ff])
        gsil = gcarve(O_GSIL, 8 * KB, "gsil")
        gsv = b16v(gsil, "p (t n) -> p t n", t=NT)
        for half in range(2):
            wv = load_w("gr%d" % half, wA[WIN_IDX["gr%d" % half], :, :], 128, 4096)
            banks, P4 = proj_quad(hT, wv, 512)
            th = gcarve(O_E + 4 * KB + half * 0, 8 * KB, "gth")
            th3 = f32v(th, "p (t n) -> p t n", t=NT)
            act(th3, P4, AF.Tanh, banks, [th], scale=0.5)
            stt(th3, th3, 1.0, P4, ALU.add, ALU.mult, [th] + list(banks), [th])
            tt("pool", gsv[:, :, half * 512:(half + 1) * 512], th3,
               f32v(g_ret)[:, half * 512:(half + 1) * 512].unsqueeze(1).broadcast_to([128, NT, 512]), ALU.mult,
               [th, g_ret], [gsil])

        ck("C")
        retT = gcarve(O_RETT, 8 * KB, "retT")
        retTv = b16v(retT, "p (c n) -> p c n", c=8)
        mskv = f32v(maskT, "p (h i) -> p h i", h=4)
        ryn = {}
        ram = {}

        def ret_a1(t):
            ts_ = slice(t * 128, (t + 1) * 128)
            pa = PS.bank(5, "AT")
            for h in range(NH_R):
                mm(pa.ap[:, h * 128:(h + 1) * 128], krTv[:, h, ts_], qrTv[:, h, ts_], True, True, [krT, qrT], [pa])
            AM = gcarve(O_WS[0] + (t % 2) * KB, KB, "AM")
            AMv = b16v(AM, "p (h i) -> p h i", h=NH_R)
            tt("dve", AMv, pa.ap.rearrange("p (h i) -> p h i", h=NH_R), mskv, ALU.mult, [pa, maskT], [AM])
            ram[t] = (AM, AMv)

        def ret_a2(t):
            ts_ = slice(t * 128, (t + 1) * 128)
            AM, AMv = ram[t]
            ybanks = [PS.bank(7, "y01"), PS.bank(5, "y23")]
            for hp in range(2):
                py = ybanks[hp]
                for hh in range(2):
                    h = hp * 2 + hh
                    o_ = py.ap[:, hh * 256:(hh + 1) * 256]
                    mm(o_, AMv[:, h, :], vrv[:, t, h * DV:(h + 1) * DV], True, False, [AM, vr], [py])
                    mm(o_, QfTv[:, h, ts_], Fb4v[:, t, h, :], False, False, [QfT, Fb4], [py])
                    mm(o_, QbTv[:, h, ts_], bngv[:, t, h * DV:(h + 1) * DV], False, True, [QbT, bng], [py])
            sF = statsF[t % 2]
            sFv = f32v(sF)
            yn = gcarve(O_WS[0] + 2 * KB + (t % 2) * 2 * KB, 2 * KB, "yn")
            ynv = b16v(yn)
            for h in range(NH_R):
                py = ybanks[h // 2]
                S.dve(lambda e, o=sFv[:, h * 6:h * 6 + 6], i_=py.ap[:, (h % 2) * 256:(h % 2 + 1) * 256]:
                      e.bn_stats(out=o, in_=i_), [py], [sF])
                S.dve(lambda e, o=sFv[:, 32 + h * 2:34 + h * 2], i_=sFv[:, h * 6:h * 6 + 6]:
                      e.bn_aggr(out=o, in_=i_), [sF], [sF])
            mvv = sFv[:, 32:40].rearrange("p (h two) -> p h two", two=2)
            ts("pool", sFv[:, 40:44], mvv[:, :, 1], EPS, ALU.add, [sF], [sF])
            tt("pool", sFv[:, 40:44], sFv[:, 40:44], smallv[:, C_NH:C_NH + 4], ALU.pow, [sF, small], [sF])
            for h in range(NH_R):
                py = ybanks[h // 2]
                ts("dve", ynv[:, h * DV:(h + 1) * DV], py.ap[:, (h % 2) * 256:(h % 2 + 1) * 256],
                   sFv[:, 32 + 2 * h:33 + 2 * h], ALU.subtract, [py, sF], [yn], s2=sFv[:, 40 + h:41 + h], op1=ALU.mult)
            ryn[t] = yn

        def ret_b(t):
            ts_ = slice(t * 128, (t + 1) * 128)
            yn = ryn[t]
            rt = gcarve(O_WS[1] + (t % 2) * 2 * KB, 2 * KB, "ret")
            tt("dve", rt[:, :], b16v(yn), gsv[:, t, :], ALU.mult, [yn, gsil], [rt])
            p2 = PS.bank(7, "trr")
            p2v = p2.ap.bitcast(BF16)
            for c in range(8):
                tr(p2v[:, c * 128:(c + 1) * 128], rt[:, c * 128:(c + 1) * 128], [rt], [p2])
            cp("dve", retTv[:, :, ts_], p2v.rearrange("p (c n) -> p c n", c=8), [p2], [retT])

        def ret_hooks(k):
            hk = {}
            if k < NT:
                hk[4] = lambda k=k: ret_a1(k)
                hk["end"] = lambda k=k: ret_a2(k)
            if 1 <= k <= NT:
                hk[6] = lambda k=k: ret_b(k - 1)
            return hk

        attT = gcarve(O_ATT, 8 * KB, "attT")
        attv = b16v(attT, "p (h n) -> p h n", h=NH_A)
        memset("pool", attv[64:128, :, :], 0.0, [attT])
        rd = gcarve(O_E + 21 * KB, 2 * KB, "rd")
        rdv = f32v(rd)
        memset("pool", rdv, 0.0, [rd])
        KBLK = min(2048, nkeys)
        nblk = nkeys // KBLK
        npair = KBLK // 256
        e_i = [0]
        nbias = smallv[:, C_NB:C_NB + 1]
        deferred = [None]
        for h in range(NH_A):
            kv = h // (NH_A // NKV)
            hooks = ret_hooks(h)
            Ob = PS.bank(6, "O")
            pend = None
            cnt = 0
            total = nblk * npair
            for blk in range(nblk):
                i = e_i[0]
                e_i[0] += 1
                ktb = gcarve(O_E + (i % 2) * 4 * KB, 4 * KB, "ktb")
                vb = gcarve(O_E + 8 * KB + (i % 2) * 2112, 2112, "vb")
                ktbv = b16v(ktb)[:, 0:KBLK]
                vbv = b16v(vb)[:, 0:(KBLK // 128) * 66].rearrange("p (t c) -> p t c", c=66)
                g0, g1 = blk * KBLK // G, (blk + 1) * KBLK // G
                dma(ktbv, kts[:, blk * KBLK:(blk + 1) * KBLK], ktT_list[g0:g1], [ktb])
                dma(vbv, vs[kv, :, (blk * KBLK // 128) * 66:((blk + 1) * KBLK // 128) * 66].rearrange("p (t c) -> p t c", c=66),
                    vT_list[g0:g1], [vb])
                for pr in range(npair):
                    b0 = 2 * (cnt % 2)
                    s0, s1 = PS.bank(b0, "S0"), PS.bank(b0 + 1, "S1")
                    for u, sb_ in enumerate((s0, s1)):
                        kt_ = pr * 2 + u
                        mm(sb_.ap[:, :], ktbv[:, kt_ * 128:(kt_ + 1) * 128], QTv[:, h, :], True, True, [ktb, QT], [sb_])
                    if pend is not None:
                        pend()
                    if deferred[0] is not None and cnt == 2:
                        deferred[0]()
                        deferred[0] = None
                    if cnt in hooks:
                        hooks.pop(cnt)()
                    Pt = gcarve(O_E + 13 * KB + (cnt % 2) * 2 * KB, 2 * KB, "P")
                    Pv = b16v(Pt, "p (u n) -> p u n", u=2)
                    act(Pv, psum_t[:, b0:b0 + 2, :], AF.Exp, [s0, s1, small], [Pt], scale=float(DH) ** -0.5, bias=nbias)

                    def pv_step(Pt=Pt, Pv=Pv, vb=vb, vbv=vbv, pr=pr, first=(cnt == 0), last=(cnt == total - 1), Ob=Ob):
                        for u in range(2):
                            mm(Ob.ap[0:65, :], vbv[:, pr * 2 + u, 0:65], Pv[:, u, :], first and u == 0, last and u == 1,
                               [vb, Pt], [Ob])
                    pend = pv_step
                    cnt += 1
            pend()
            if deferred[0] is not None:
                deferred[0]()
                deferred[0] = None
            Ou = gcarve(O_E + 17 * KB + (h % 2) * 2 * KB, 2 * KB, "Ou")
            Ouv = f32v(Ou)
            cp("act", Ouv[0:65, :], Ob.ap[0:65, :], [Ob], [Ou])
            S.dve(lambda e, o=rdv[64:65, :], i_=Ouv[64:65, :]: e.reciprocal(out=o, in_=i_), [Ou], [rd])

            def fin(h=h, Ou=Ou, Ouv=Ouv):
                bc = PS.bank(4, "bc")
                mm(bc.ap[0:64, :], f32v(ones_f)[:, 0:64], rdv[:, :], True, True, [ones_f, rd], [bc])
                tt("dve", attv[0:64, h, :], Ouv[0:64, :], bc.ap[0:64, :], ALU.mult, [Ou, bc], [attT])
            deferred[0] = fin
            for key in (4, 6, "end"):
                if key in hooks:
                    hooks.pop(key)()

        if deferred[0] is not None:
            deferred[0]()
            deferred[0] = None
        ck("E")
        ck("F")
        tg = gcarve(O_TG, 16 * KB, "tg")
        tgv = b16v(tg, "p (t n) -> p t n", t=NT)
        for blk in range(4):
            wv = load_w("gt%d" % blk, wA[WIN_IDX["gt%d" % blk], :, :], 128, 4096)
            for t in range(NT):
                pb = proj_block(hT, wv, 512, t,
                                extra=(ones_b[:, 0:128], bgb[:, blk * 512:(blk + 1) * 512], [ones_b, bgb]))
                act(tgv[:, t, blk * 512:(blk + 1) * 512], pb.ap[:, :], AF.Tanh, [pb], [tg], scale=0.5)
        mix = gcarve(O_MIX, 8 * KB, "mix")
        mixv = b16v(mix, "p (t n) -> p t n", t=NT)
        for half in range(2):
            wbr_t, wbr_v = load_w("wbr_%d" % half, wbrs[half, :, :], 128, 4096)
            wba_t, wba_v = load_w("wba_%d" % half, wbas[half, :, :], 64, 4096)
            wba_v = wba_t.ap[:, 0:4096]
            wbr3 = wbr_v.rearrange("p (c n) -> p c n", c=8)
            wba3 = wba_v.rearrange("p (c n) -> p c n", c=8)
            hs = slice(half * 512, (half + 1) * 512)
            for t in range(NT):
                ts_ = slice(t * 128, (t + 1) * 128)
                pr_ = PS.next("br")
                for c in range(8):
                    mm(pr_.ap[:, :], retTv[:, c, ts_], wbr3[:, c, :], c == 0, c == 7, [retT, wbr_t], [pr_])
                pa_ = PS.next("ba")
                for h in range(NH_A):
                    mm(pa_.ap[:, :], attv[:, h, ts_], wba3[:, h, :], h == 0, h == NH_A - 1, [attT, wba_t], [pa_])
                m1 = gcarve(O_M1 + (t % 2) * 2 * KB, 2 * KB, "m1")
                stt(f32v(m1), tgv[:, t, half * 512:(half + 1) * 512], 1.0, pa_.ap[:, :], ALU.add, ALU.mult, [tg, pa_], [m1])
                m2 = gcarve(O_M1 + 4 * KB + (t % 2) * 2 * KB, 2 * KB, "m2")
                stt(f32v(m2), tgv[:, t, 1024 + half * 512:1024 + (half + 1) * 512], 1.0, pr_.ap[:, :], ALU.add, ALU.mult,
                    [tg, pr_], [m2])
                tt("pool", mixv[:, t, hs], f32v(m1), f32v(m2), ALU.add, [m1, m2], [mix])
        mixT = gcarve(O_MIXT, 8 * KB, "mixT")
        mixTv = b16v(mixT, "p (c n) -> p c n", c=8)
        for t in range(NT):
            p2 = PS.next("trm")
            p2v = p2.ap.bitcast(BF16)
            for c in range(8):
                tr(p2v[:, c * 128:(c + 1) * 128], mixv[:, t, c * 128:(c + 1) * 128], [mix], [p2])
            cp(evac_eng(), mixTv[:, :, t * 128:(t + 1) * 128], p2v.rearrange("p (c n) -> p c n", c=8), [p2], [mixT])
        for half in range(2):
            wo_t, wo_v = load_w("wout_%d" % half, wouts[half, :, :], 128, 4096)
            wo3 = wo_v.rearrange("p (c n) -> p c n", c=8)
            for t in range(NT):
                po = PS.next("wo")
                for c in range(8):
                    mm(po.ap[:, :], mixTv[:, c, t * 128:(t + 1) * 128], wo3[:, c, :], c == 0, c == 7, [mixT, wo_t], [po])
                stt(xv[:, t, half * 512:(half + 1) * 512], po.ap[:, :], 0.5, xv[:, t, half * 512:(half + 1) * 512],
                    ALU.mult, ALU.add, [po, xt], [xt])
        ck("G")
        h2T = gcarve(O_HT, 8 * KB, "h2T")
        norm_transpose(xt, g_ffn, h2T, 8)
        h2v = b16v(h2T, "p (c n) -> p c n", c=8)
        hid = gcarve(O_HID, 22 * KB, "hid")
        hidv = b16v(hid, "p (f n) -> p f n", f=NFC)
        ft_i = 0
        for fb in range(11):
            w1_t, w1_v = load_w("w1_%d" % fb, w1s[fb, :, :], 128, 4096)
            w1v = w1_v.rearrange("p (u c n) -> p u c n", u=2, c=8)
            for fc in range(2):
                f = fb * 2 + fc
                pg = PS.next("fg")
                pu = PS.next("fu")
                for c in range(8):
                    mm(pg.ap[:, :], w1v[:, 0, c, fc * 128:(fc + 1) * 128], h2v[:, c, :], c == 0, c == 7, [w1_t, h2T], [pg])
                for c in range(8):
                    mm(pu.ap[:, :], w1v[:, 1, c, fc * 128:(fc + 1) * 128], h2v[:, c, :], c == 0, c == 7, [w1_t, h2T], [pu])
                th = gcarve(O_FT + (ft_i % 2) * 2 * KB, 2 * KB, "fth")
                a_ = gcarve(O_FT + 4 * KB + (ft_i % 2) * 2 * KB, 2 * KB, "fa")
                ft_i += 1
                act(f32v(th), pg.ap[:, :], AF.Tanh, [pg], [th], scale=0.5)
                stt(f32v(a_), f32v(th), 1.0, pg.ap[:, :], ALU.add, ALU.mult, [th, pg], [a_])
                tt("dve", hidv[:, f, :], f32v(a_), pu.ap[:, :], ALU.mult, [a_, pu], [hid])
        ck("I")
        for half in range(2):
            w2t = gcarve(O_W2[half], 22 * KB, "w2")
            w2v = b16v(w2t, "p (f n) -> p f n", f=NFC)
            dma(b16v(w2t), w2s[half, :, :], wT["w2_%d" % half], [w2t])
            for t in range(NT):
                po = PS.next("f2")
                for f in range(NFC):
                    mm(po.ap[:, :], hidv[:, f, t * 128:(t + 1) * 128], w2v[:, f, :], f == 0, f == NFC - 1, [hid, w2t], [po])
                stt(xv[:, t, half * 512:(half + 1) * 512], po.ap[:, :], 0.5, xv[:, t, half * 512:(half + 1) * 512],
                    ALU.mult, ALU.add, [po, xt], [xt])
        ots = []
        for t in range(NT):
            ot = gcarve(O_OUT[t % 2] if t < 2 else O_FT + (t - 2) * 4 * KB, 4 * KB, "ot")
            ots.append(ot)
            act(ot.ap[:, 0:1024], xv[:, t, :], AF.Square, [xt], [ot, stats], accum=sv[:, 16 + t:17 + t])
        rstd_from_ss(sv[:, 16:16 + NT], sv[:, 20:20 + NT], NT, float(D), [stats], [stats])
        for t in range(NT):
            ot = ots[t]
            stt(f32v(ot), xv[:, t, :], sv[:, 20 + t:21 + t], f32v(g_fin), ALU.mult, ALU.mult, [xt, stats, g_fin], [ot])
            dma(out_ap[(tile0 + t) * 128:(tile0 + t + 1) * 128, :], f32v(ot), [ot], [], q="pool")

    def run_pass1(x_ap, tab_ap, ng, k_list, v_list, bnT, bpar):
        order = list(reversed(range(ng)))
        xts = {}
        for k in range(min(2, ng)):
            xts[k] = load_x(x_ap, order[k] * NT, k % 2)
        gens = [pass1_gen(x_ap, tab_ap, order[k], k, 0, k_list, v_list, bnT, bpar, xts) for k in range(ng)]

        def after_a(k):
            if k + 2 < ng:
                xts[k + 2] = load_x(x_ap, order[k + 2] * NT, k % 2)
        pipeline(gens, ng, after_a, lambda k: None)

    def run_sequence(x_ap, tab_ap, out_ap, ntok, is_sample):
        ng = ntok // G
        ktT_list, vT_list, bnT = [], [], {}
        bpar, fpar = [0], [0]
        if is_sample:
            memset("pool", f32v(Ff), 0.0, [Ff])
            memset("pool", f32v(Bf), 0.0, [Bf])
            nog = (TS - SEG) // G
            own_k, own_v = [], []
            oth_k, oth_v = [], []
            xts = {}
            t0o = SEG // 128
            for og in range(min(2, nog)):
                xts[og] = load_x(xs, t0o + og * NT, og % 2)
            gens = [others_gen(og, oth_k, oth_v, xts) for og in range(nog)]

            def after_a(og):
                if og + 2 < nog:
                    xts[og + 2] = load_x(xs, t0o + (og + 2) * NT, og % 2)
                if og % 2 == 1:
                    prep_step()
            pipeline(gens, nog, after_a, lambda og: prep_step())
            run_prep(len(prep_tasks) + 1)
            cp("act", Bbf[0][:, :], f32v(Bf), [Bf], [Bbf[0]])
            run_pass1(x_ap, tab_ap, ng, own_k, own_v, bnT, bpar)
            own_k.reverse()
            own_v.reverse()
            ktT_list = own_k + oth_k
            vT_list = own_v + oth_v
            nkeys = TS
        else:
            memset("pool", f32v(Ff), 0.0, [Ff])
            memset("pool", f32v(Bf), 0.0, [Bf])
            memset("pool", Bbf[0][:, :], 0.0, [Bbf[0]])
            run_prep(len(prep_tasks) + 1)
            ck("prepall")
            run_pass1(x_ap, tab_ap, ng, ktT_list, vT_list, bnT, bpar)
            ck("pass1")
            ktT_list.reverse()
            vT_list.reverse()
            nkeys = ntok
        cp("act", Fbf[0][:, :], f32v(Ff), [Ff], [Fbf[0]])
        for g in range(ng):
            pass2_group(x_ap, tab_ap, out_ap, g, nkeys, ktT_list, vT_list, bnT, fpar)

    try:
      setup()
      gen_tables()
      ck("setup")
      make_prep(["kva", "kr", "vr0", "vr1", "qa", "qr", "gr0", "gr1", "gt0", "gt1", "gt2", "gt3",
               "wbr_0", "wba_0", "wbr_1", "wba_1", "wout_0", "wout_1"] +
                ["w1_%d" % i for i in range(11)] + ["w2_0", "w2_1"])
      run_prep(4)
      ck("prep4")
      if cfg.get("do_sample", True):
        run_sequence(xs, tabs, ys, SEG, True)
      for i in range(NP):
        run_sequence(xp[i], tabp, yp[i], TP, False)
    except StopBuild:
      pass

    with nc.Block() as block:
        @block.sync
        def _(sync):
            S.emit(st)
    st.close()
    return nc, S


def rowcol(pos):
    pos = np.asarray(pos)
    rc = np.stack([pos // GW, pos % GW], axis=-1).astype(np.float32)
    return rc.reshape(-1, 128, 2).transpose(1, 0, 2)


def const_table():
    j = np.arange(128, dtype=np.float32)[:, None]
    i = np.arange(128, dtype=np.float32)[None, :]
    c = np.zeros((128, 802), np.float32)
    c[:, 770:802] = np.arange(32, dtype=np.float32)[None, :]
    c[:, 0:128] = np.maximum(i - j, 0)
    c[:, 128:256] = np.maximum(j - i, 0)
    c[:, 256:384] = (i >= j)
    c[:, 384:512] = (i < j)
    c[:, 512:640] = i + 1
    c[:, 640:768] = 128 - i
    c[:, 768] = 127 - j[:, 0]
    c[:, 769] = j[:, 0]
    return c


def host_inputs(cfg, core, inputs):
    TP, NP, TS, SEG = cfg["TP"], cfg["NP"], cfg["TS"], cfg["SEG"]
    f = lambda a: np.ascontiguousarray(np.asarray(a, dtype=np.float32))
    roll = core * SEG
    xs_full = np.asarray(inputs["x_sample"], dtype=np.float32)[0]
    pos = (np.arange(TS) + roll) % TS
    m = {
        "xp": f(np.asarray(inputs["x_prompt"])[core * NP:(core + 1) * NP]),
        "xs": f(np.roll(xs_full, -roll, axis=0)),
        "rcp": f(rowcol(np.arange(TP))),
        "rcs": f(rowcol(pos)),
        "ctab": const_table(),
    }
    df = np.where(pos < roll, roll - 1 - pos, BIG).astype(np.float32)
    db = np.where(pos >= roll + SEG, pos - (roll + SEG), BIG).astype(np.float32)
    own = (pos >= roll) & (pos < roll + SEG)
    df[own] = BIG
    db[own] = BIG
    d2 = np.stack([df, db], axis=-1).reshape(TS // 128, 128, 2).transpose(1, 0, 2)
    m["dists"] = f(d2)
    m["w_in"] = f(inputs["w_in"][0])
    m["b_gate"] = f(inputs["b_gate"][0]).reshape(1, 2048)
    m["q_norm"] = f(inputs["q_norm"][0]).reshape(1, DH)
    m["k_norm"] = f(inputs["k_norm"][0]).reshape(1, DH)
    m["dec_f"] = f(inputs["ret_decay_fwd"][0]).reshape(1, 4)
    m["dec_b"] = f(inputs["ret_decay_bwd"][0]).reshape(1, 4)
    m["ret_norm"] = f(inputs["ret_norm"][0]).reshape(1, D)
    m["w_ba"] = f(inputs["w_branch_attn"][0])
    m["w_br"] = f(inputs["w_branch_ret"][0])
    m["w_out"] = f(inputs["w_out"][0])
    m["norm_mix"] = f(inputs["norm_mix"][0]).reshape(1, D)
    m["norm_ffn"] = f(inputs["norm_ffn"][0]).reshape(1, D)
    m["norm_fin"] = f(inputs["norm_final"]).reshape(1, D)
    m["w_f1"] = f(inputs["w_ffn_in"][0])
    m["w_f2"] = f(inputs["w_ffn_out"][0])
    return m


_CACHE = {}


def run(cfg, inputs, ncores=8):
    key = tuple(sorted(cfg.items()))
    if key not in _CACHE:
        _CACHE[key] = build(cfg)[0]
    nc = _CACHE[key]
    in_maps = [host_inputs(cfg, c, inputs) for c in range(ncores)]
    res = run_bass_kernel_spmd(nc, in_maps, core_ids=list(range(ncores)))
    yp = np.concatenate([res.results[c]["yp"] for c in range(ncores)], axis=0)
    ysm = np.concatenate([res.results[c]["ys"] for c in range(ncores)], axis=0)[None]
    return yp.astype(np.float32), ysm.astype(np.float32)


def kernel(**inputs):
    cfg = {"TP": 2048, "NP": 2, "TS": 16384, "SEG": 2048}
    return run(cfg, inputs)
```
